# Optimizing a Trainium2 kernel written in Bass

```python
import math
import jax, jax.numpy as jnp
from jax import lax
import numpy as np

D_MODEL = 1024
BATCH = 16
SEQ = 2048
DEPTH = 1

MIX_WIDTH = D_MODEL
DIFF_HEADS = 4
DIFF_HEAD_DIM = 64
DIFF_V_DIM = 2 * DIFF_HEAD_DIM
DIFF_WIDTH = DIFF_HEADS * DIFF_V_DIM
MLA_HEADS = 4
MLA_NOPE_DIM = 64
MLA_ROPE_DIM = 32
MLA_QK_DIM = MLA_NOPE_DIM + MLA_ROPE_DIM
MLA_V_DIM = (MIX_WIDTH - DIFF_WIDTH) // MLA_HEADS
MLA_WIDTH = MLA_HEADS * MLA_V_DIM
MLA_Q_RANK = 384
MLA_KV_RANK = 256
DIFF_QK_COLS = DIFF_HEADS * 2 * DIFF_HEAD_DIM
DIFF_V_COLS = DIFF_WIDTH
IN_SPLITS = (DIFF_QK_COLS,
             2 * DIFF_QK_COLS,
             2 * DIFF_QK_COLS + DIFF_V_COLS,
             2 * DIFF_QK_COLS + DIFF_V_COLS + MLA_Q_RANK,
             2 * DIFF_QK_COLS + DIFF_V_COLS + MLA_Q_RANK + MLA_KV_RANK)
IN_COLS = 2 * DIFF_QK_COLS + DIFF_V_COLS + MLA_Q_RANK + MLA_KV_RANK + MLA_ROPE_DIM
D_FF = 2816
ROPE_THETA = 10000.0
NORM_EPS = 1e-6
Q_BLOCK = 128
N_MOD = 9

kernel_name = "hymba_diffattn_mla_macaron_adaln"


def lambda_init_fn(layer_idx):
    return 0.8 - 0.6 * math.exp(-0.3 * layer_idx)


def rmsnorm(x, g):
    x32 = x.astype(jnp.float32)
    y = x32 * lax.rsqrt(jnp.mean(x32 * x32, axis=-1, keepdims=True) + NORM_EPS)
    return y.astype(x.dtype) * g


def modulate(xn, shift, scale):
    return xn * (1 + scale) + shift


def swiglu(x, w_gate, w_up, w_down):
    return (jax.nn.silu(x @ w_gate) * (x @ w_up)) @ w_down


def rope_tables(positions, dim):
    inv_freq = ROPE_THETA ** (-jnp.arange(0, dim, 2, dtype=jnp.float32) / dim)
    ang = positions.astype(jnp.float32)[..., None] * inv_freq
    return jnp.cos(ang), jnp.sin(ang)


def apply_rope(x, cos, sin):
    shape = cos.shape[:2] + (1,) * (x.ndim - 3) + cos.shape[-1:]
    cos = cos.reshape(shape).astype(x.dtype)
    sin = sin.reshape(shape).astype(x.dtype)
    x1, x2 = jnp.split(x, 2, axis=-1)
    return jnp.concatenate([x1 * cos - x2 * sin, x2 * cos + x1 * sin], axis=-1)


def causal_block_attention(q, k, v, scale, combine):
    S = q.shape[1]
    outs = []
    for i in range(S // Q_BLOCK):
        q0 = i * Q_BLOCK
        kend = q0 + Q_BLOCK
        qb = q[:, q0:kend]
        kb = k[:, :kend]
        vb = v[:, :kend]
        s = jnp.einsum('bqhmd,bkhmd->bhmqk', qb, kb,
                       preferred_element_type=jnp.float32) * scale
        causal = (q0 + jnp.arange(Q_BLOCK))[:, None] >= jnp.arange(kend)[None, :]
        s = jnp.where(causal, s, jnp.finfo(jnp.float32).min)
        p = jax.nn.softmax(s, axis=-1)
        w = combine(p).astype(v.dtype)
        outs.append(jnp.einsum('bhqk,bkhd->bqhd', w, vb))
    return jnp.concatenate(outs, axis=1)


def hybrid_mixer(h, cos_d, sin_d, cos_r, sin_r, layer_idx,
                 w_in, lq1, lk1, lq2, lk2, diff_subln,
                 mla_q_norm, mla_w_uq, mla_kv_norm, mla_w_ukv, mla_out_norm, w_out):
    B, S, _ = h.shape
    proj = h @ w_in
    dq, dk, dv, cq, ckv, kr = jnp.split(proj, IN_SPLITS, axis=-1)

    dq = apply_rope(dq.reshape(B, S, DIFF_HEADS, 2, DIFF_HEAD_DIM), cos_d, sin_d)
    dk = apply_rope(dk.reshape(B, S, DIFF_HEADS, 2, DIFF_HEAD_DIM), cos_d, sin_d)
    dv = dv.reshape(B, S, DIFF_HEADS, DIFF_V_DIM)
    lam_init = lambda_init_fn(layer_idx)
    f32 = jnp.float32
    lam = (jnp.exp(jnp.sum(lq1.astype(f32) * lk1.astype(f32)))
           - jnp.exp(jnp.sum(lq2.astype(f32) * lk2.astype(f32))) + lam_init)
    od = causal_block_attention(dq, dk, dv, DIFF_HEAD_DIM ** -0.5,
                                lambda p: p[:, :, 0] - lam * p[:, :, 1])
    od = rmsnorm(od, diff_subln) * (1 - lam_init)
    od = od.reshape(B, S, DIFF_WIDTH)

    cq = rmsnorm(cq, mla_q_norm)
    q = (cq @ mla_w_uq).reshape(B, S, MLA_HEADS, MLA_QK_DIM)
    q_nope, q_rope = jnp.split(q, [MLA_NOPE_DIM], axis=-1)
    q_rope = apply_rope(q_rope, cos_r, sin_r)
    ckv = rmsnorm(ckv, mla_kv_norm)
    kv = (ckv @ mla_w_ukv).reshape(B, S, MLA_HEADS, MLA_NOPE_DIM + MLA_V_DIM)
    k_nope, mv = jnp.split(kv, [MLA_NOPE_DIM], axis=-1)
    kr = apply_rope(kr, cos_r, sin_r)
    kr = jnp.broadcast_to(kr[:, :, None, :], (B, S, MLA_HEADS, MLA_ROPE_DIM))
    qm = jnp.concatenate([q_nope, q_rope], axis=-1)[:, :, :, None, :]
    km = jnp.concatenate([k_nope, kr], axis=-1)[:, :, :, None, :]
    om = causal_block_attention(qm, km, mv, MLA_QK_DIM ** -0.5, lambda p: p[:, :, 0])
    om = rmsnorm(om.reshape(B, S, MLA_WIDTH), mla_out_norm)

    return jnp.concatenate([od, om], axis=-1) @ w_out


def setup_inputs(seed: int = 0) -> dict:
    key = jax.random.key(seed)
    ks = iter(jax.random.split(key, 40))
    nrm = lambda shape, s: jax.random.normal(next(ks), shape, jnp.float32) * s
    gain = lambda shape: 1.0 + nrm(shape, 0.02)
    D, L = D_MODEL, DEPTH
    offs = jax.random.randint(next(ks), (BATCH, 1), 0, 1024, dtype=jnp.int32)
    positions = offs + jnp.arange(SEQ, dtype=jnp.int32)[None, :]
    return {
        "x": nrm((BATCH, SEQ, D), 1.0),
        "c": nrm((BATCH, D), 1.0),
        "positions": positions,
        "w_ada": nrm((L, D, N_MOD * D), D ** -0.5),
        "b_ada": nrm((L, N_MOD * D), 0.02),
        "ffn1_norm": gain((L, D)),
        "ffn1_w_gate": nrm((L, D, D_FF), D ** -0.5),
        "ffn1_w_up": nrm((L, D, D_FF), D ** -0.5),
        "ffn1_w_down": nrm((L, D_FF, D), D_FF ** -0.5),
        "mix_norm": gain((L, D)),
        "w_in": nrm((L, D, IN_COLS), D ** -0.5),
        "diff_lambda_q1": nrm((L, DIFF_HEAD_DIM), 0.1),
        "diff_lambda_k1": nrm((L, DIFF_HEAD_DIM), 0.1),
        "diff_lambda_q2": nrm((L, DIFF_HEAD_DIM), 0.1),
        "diff_lambda_k2": nrm((L, DIFF_HEAD_DIM), 0.1),
        "diff_subln": gain((L, DIFF_V_DIM)),
        "mla_q_norm": gain((L, MLA_Q_RANK)),
        "mla_w_uq": nrm((L, MLA_Q_RANK, MLA_HEADS * MLA_QK_DIM), MLA_Q_RANK ** -0.5),
        "mla_kv_norm": gain((L, MLA_KV_RANK)),
        "mla_w_ukv": nrm((L, MLA_KV_RANK, MLA_HEADS * (MLA_NOPE_DIM + MLA_V_DIM)), MLA_KV_RANK ** -0.5),
        "mla_out_norm": gain((L, MLA_WIDTH)),
        "w_out": nrm((L, MIX_WIDTH, D), MIX_WIDTH ** -0.5),
        "ffn2_norm": gain((L, D)),
        "ffn2_w_gate": nrm((L, D, D_FF), D ** -0.5),
        "ffn2_w_up": nrm((L, D, D_FF), D ** -0.5),
        "ffn2_w_down": nrm((L, D_FF, D), D_FF ** -0.5),
        "final_norm": gain((D,)),
    }


def reference(x, c, positions, w_ada, b_ada,
              ffn1_norm, ffn1_w_gate, ffn1_w_up, ffn1_w_down,
              mix_norm, w_in, diff_lambda_q1, diff_lambda_k1, diff_lambda_q2, diff_lambda_k2,
              diff_subln, mla_q_norm, mla_w_uq, mla_kv_norm, mla_w_ukv, mla_out_norm, w_out,
              ffn2_norm, ffn2_w_gate, ffn2_w_up, ffn2_w_down, final_norm):
    B, S, D = x.shape
    cos_d, sin_d = rope_tables(positions, DIFF_HEAD_DIM)
    cos_r, sin_r = rope_tables(positions, MLA_ROPE_DIM)
    c_act = jax.nn.silu(c)
    h = x
    for l in range(DEPTH):
        mod = (c_act @ w_ada[l] + b_ada[l]).reshape(B, N_MOD, D)[:, :, None, :]
        sh1, sc1, g1, sh2, sc2, g2, sh3, sc3, g3 = [mod[:, j] for j in range(N_MOD)]
        u = modulate(rmsnorm(h, ffn1_norm[l]), sh1, sc1)
        h = h + 0.5 * g1 * swiglu(u, ffn1_w_gate[l], ffn1_w_up[l], ffn1_w_down[l])
        u = modulate(rmsnorm(h, mix_norm[l]), sh2, sc2)
        h = h + g2 * hybrid_mixer(u, cos_d, sin_d, cos_r, sin_r, l,
                                  w_in[l], diff_lambda_q1[l], diff_lambda_k1[l],
                                  diff_lambda_q2[l], diff_lambda_k2[l], diff_subln[l],
                                  mla_q_norm[l], mla_w_uq[l], mla_kv_norm[l], mla_w_ukv[l],
                                  mla_out_norm[l], w_out[l])
        u = modulate(rmsnorm(h, ffn2_norm[l]), sh3, sc3)
        h = h + 0.5 * g3 * swiglu(u, ffn2_w_gate[l], ffn2_w_up[l], ffn2_w_down[l])
    return rmsnorm(h, final_norm)
```

```python
import math
import contextlib
import numpy as np
import concourse.bass as bass
import concourse.mybir as mybir
from concourse.bass_utils import run_bass_kernel_spmd

F32 = mybir.dt.float32
BF16 = mybir.dt.bfloat16
I32 = mybir.dt.int32
AF = mybir.ActivationFunctionType
ALU = mybir.AluOpType

ENGS = ("pe", "act", "dve", "pool", "sp")
STRICT_SAME_ENGINE = True


class Res:
    __slots__ = ("name", "w", "rs", "excl")

    def __init__(self, name, excl=False):
        self.name = name
        self.w = None
        self.rs = []
        self.excl = excl


class Op:
    __slots__ = ("eng", "fn", "deps", "inc", "tick", "dma", "dsem", "dval")

    def __init__(self, eng, fn, dma):
        self.eng = eng
        self.fn = fn
        self.deps = []
        self.inc = False
        self.tick = 0
        self.dma = dma
        self.dsem = None
        self.dval = 0


class Sched:
    def __init__(self, nc):
        self.nc = nc
        self.ops = {e: [] for e in ENGS}
        self.streams = {}

    def _dep(self, op, p, kind):
        if p is None or p is op:
            return
        if not p.dma and not op.dma and p.eng == op.eng:
            if op.eng == "pe" or (kind != "raw" and not STRICT_SAME_ENGINE):
                return
        op.deps.append(p)
        if not p.dma:
            p.inc = True

    def op(self, eng, name, r=(), w=(), dma=False, stream=None, **kw):
        o = Op(eng, (name, kw), dma)
        def last_readers(rs):
            last = {}
            out = []
            for q in rs:
                if q.dma:
                    out.append(q)
                else:
                    last[q.eng] = q
            return out + list(last.values())

        for x in r:
            self._dep(o, x.w, "raw")
            if x.excl:
                for q in last_readers(x.rs):
                    if q.eng != eng or q.dma:
                        self._dep(o, q, "raw")
        for x in w:
            self._dep(o, x.w, "waw")
            for q in last_readers(x.rs):
                self._dep(o, q, "war")
        for x in r:
            x.rs.append(o)
        for x in w:
            x.w = o
            x.rs = []
        if dma:
            st = self.streams.setdefault(stream, [0])
            st[0] += 16
            o.dsem = stream
            o.dval = st[0]
        self.ops[eng].append(o)
        return o

    def dma(self, eng, r=(), w=(), stream=None, **kw):
        return self.op(eng, "dma_start", r, w, dma=True, stream=stream, **kw)

    def emit(self, final_waits=()):
        nc = self.nc
        for e in ENGS:
            t = 0
            for o in self.ops[e]:
                if not o.dma and o.inc:
                    t += 1
                    o.tick = t
        with contextlib.ExitStack() as es:
            esem = {e: es.enter_context(nc.semaphore("s_" + e)) for e in ENGS if e != "sp"}
            dsem = {name: es.enter_context(nc.semaphore("d_" + str(name))) for name in self.streams}
            block = es.enter_context(nc.Block())

            def run(ename, eng):
                seen = {}
                for o in self.ops[ename]:
                    for p in o.deps:
                        if p.dma:
                            key, val, sem = ("d", p.dsem), p.dval, dsem[p.dsem]
                        else:
                            key, val, sem = ("e", p.eng), p.tick, esem[p.eng]
                        if seen.get(key, 0) < val:
                            eng.wait_ge(sem, val)
                            seen[key] = val
                    ins = getattr(eng, o.fn[0])(**o.fn[1])
                    if o.dma:
                        ins.then_inc(dsem[o.dsem], 16)
                    elif o.inc:
                        ins.then_inc(esem[ename], 1)
                if ename == "sp":
                    for p in final_waits:
                        eng.wait_ge(dsem[p.dsem], p.dval)

            block.tensor(lambda eng: run("pe", eng))
            block.scalar(lambda eng: run("act", eng))
            block.vector(lambda eng: run("dve", eng))
            block.gpsimd(lambda eng: run("pool", eng))
            block.sync(lambda eng: run("sp", eng))


D = 1024
T = 2048
FF = 2816
NK = 8
NTT = 4
EPS = 1e-6
TWO_PI = 2.0 * math.pi
C1 = 6.28125
C2 = TWO_PI - C1
FFN_GROUPS = [(0, 4), (4, 8), (8, 12), (12, 16), (16, 20), (20, 22)]
LAM_INIT = 0.8 - 0.6 * math.exp(0.0)
NWB = 5
MIXSTOP = [0]


def build(stages=("ffn1", "mix", "ffn2"), nseq=2, debug=False):
    nc = bass.Bass("TRN2", target_bir_lowering=False)

    def din(name, shape, dt=F32):
        return nc.dram_tensor(name, list(shape), dt, kind="ExternalInput").ap()

    xT = din("xT", [2, D, T])
    cT = din("cT", [128, 8, 2])
    pos = din("pos", [2, T], I32)
    w_ada = din("w_ada", [D, 9 * D])
    b_adaT = din("b_adaT", [128, 72])
    gains = din("gains", [128, 4, 8])
    wgs = [din("wg1", [D, FF]), din("wg2", [D, FF])]
    wus = [din("wu1", [D, FF]), din("wu2", [D, FF])]
    wds = [din("wd1", [FF, D]), din("wd2", [FF, D])]
    w_in = din("w_in", [D, 2208])
    w_out = din("w_out", [D, D])
    w_uq = din("w_uq", [384, 384])
    w_ukv = din("w_ukv", [256, 768])
    lamv = din("lamv", [4, 64])
    gsmall = din("gsmall", [128, 10])
    cmats = din("cmats", [4, 128, 128])
    invf = din("invf", [128, 2])
    outT = nc.dram_tensor("outT", [2, D, T], F32, kind="ExternalOutput").ap()
    dbg = nc.dram_tensor("dbg", [128, 144], F32, kind="ExternalOutput").ap() if debug else None

    with contextlib.ExitStack() as es:
        S = Sched(nc)

        def sb(name, shape, dt):
            return es.enter_context(nc.sbuf_tensor(name, list(shape), dt))

        def tile(name, shape, dt):
            return sb(name, shape, dt), Res(name)

        hT = sb("hT", [128, NK, T], F32)
        hR = [[Res(f"h{k}_{t}") for t in range(NTT)] for k in range(NK)]
        uT = sb("uT", [128, NK, T], BF16)
        uR = [[Res(f"u{k}_{t}") for t in range(NTT)] for k in range(NK)]
        BIG = [sb("bigA", [128, 8256], BF16), sb("bigB", [128, 8256], BF16)]
        bigR = [[Res("A")], [Res("Blo"), Res("Bhi")]]
        WB = [sb(f"wb{i}", [128, 4096], BF16) for i in range(NWB)]
        WR = [Res(f"wb{i}") for i in range(NWB)]
        PSALL = es.enter_context(nc.psum_tensor("psall", [128, 4096], F32))
        PS = [PSALL[:, i * 512:(i + 1) * 512] for i in range(8)]
        PR = [Res(f"ps{i}", excl=True) for i in range(8)]

        class Rot:
            def __init__(self, items):
                self.items = items
                self.i = 0

            def next(self):
                x = self.items[self.i % len(self.items)]
                self.i += 1
                return x

        ftmp = Rot([tile(f"ft{i}", [128, 512], F32) for i in range(4)])
        rstmp = Rot([tile(f"rs{i}", [128, 512], F32) for i in range(2)])
        btmp = Rot([tile(f"bt{i}", [128, 512], BF16) for i in range(4)])
        ptbuf = sb("ptbuf", [128, 2048], BF16)
        ptR4 = [Res(f"pt{i}") for i in range(4)]

        class PtRot:
            def __init__(self):
                self.i = 0
                self.wide = False

            def next(self):
                k = self.i
                self.i += 1
                if self.wide:
                    k %= 2
                    return ptbuf[:, k * 1024:(k + 1) * 1024], ptR4[k]
                k %= 4
                return ptbuf[:, k * 512:(k + 1) * 512], ptR4[k]
        ptp = PtRot()
        itmp, itmpR = tile("itmp", [128, 512], I32)
        posi, posiR = tile("posi", [128, 512], I32)
        accsb, accsbR = tile("accsb", [128, 4, 258], F32)
        osb = Rot([tile(f"osb{i}", [128, 128], F32) for i in range(4)])
        obf = Rot([tile(f"obf{i}", [128, 128], BF16) for i in range(4)])
        junk, junkR = tile("junk", [128, 128], F32)
        small, smallR = tile("small", [128, 32], F32)

        cm_sb, cmR = tile("cm_sb", [128, 4, 128], BF16)
        ones_sb, onesR = tile("ones_sb", [128, 128], BF16)
        invf_sb, invfR = tile("invf_sb", [128, 2], F32)
        gains_sb, gainsR = tile("gains_sb", [128, 4, 8], F32)
        gsm_sb, gsmR = tile("gsm_sb", [128, 10], F32)
        bada_sb, badaR = tile("bada_sb", [128, 72], F32)
        cT_sb, cTR = tile("cT_sb", [128, 8, 2], F32)
        cab, cabR = tile("cab", [128, 8, 2], BF16)
        modT, modR = tile("modT", [128, 72, 2], F32)
        SC, scR = tile("SC", [128, 3, 3, 2, 8], F32)
        lam_sb, lamR = tile("lam_sb", [128, 4, 64], F32)
        lamc, lamcR = tile("lamc", [128, 8], F32)

        ident = cm_sb[:, 0, :]
        rdiff = cm_sb[:, 1, :]
        rmla = cm_sb[:, 2, :]
        maskT = cm_sb[:, 3, :]

        psall = Rot(list(range(8)))

        I = S.op
        AX = mybir.AxisListType.X
        S.dma("pool", w=[cmR], stream="c0", out=cm_sb[:], in_=cmats.rearrange("c p n -> p c n"))
        S.dma("sp", w=[invfR], stream="c1", out=invf_sb[:], in_=invf)
        S.dma("sp", w=[gainsR], stream="c2", out=gains_sb[:], in_=gains)
        S.dma("sp", w=[gsmR], stream="c3", out=gsm_sb[:], in_=gsmall)
        S.dma("sp", w=[badaR], stream="c4", out=bada_sb[:], in_=b_adaT)
        S.dma("sp", w=[cTR], stream="c5", out=cT_sb[:], in_=cT)
        for i in range(4):
            S.dma("sp", w=[lamR], stream="c6", out=lam_sb[:, i, :], in_=lamv[i:i + 1, :].broadcast_to([128, 64]))
        I("dve", "memset", w=[onesR], ap=ones_sb[:], constant=1.0)
        I("dve", "tensor_tensor", r=[lamR], w=[lamR], out=lam_sb[:, 0, :], in0=lam_sb[:, 0, :], in1=lam_sb[:, 1, :], op=ALU.mult)
        I("dve", "tensor_tensor", r=[lamR], w=[lamR], out=lam_sb[:, 2, :], in0=lam_sb[:, 2, :], in1=lam_sb[:, 3, :], op=ALU.mult)
        I("dve", "reduce_sum", r=[lamR], w=[lamcR], out=lamc[:, 1:2], in_=lam_sb[:, 0, :], axis=AX)
        I("dve", "reduce_sum", r=[lamR], w=[lamcR], out=lamc[:, 2:3], in_=lam_sb[:, 2, :], axis=AX)
        I("act", "activation", r=[lamcR], w=[lamcR], out=lamc[:, 3:5], in_=lamc[:, 1:3], func=AF.Exp)
        I("dve", "scalar_tensor_tensor", r=[lamcR], w=[lamcR], out=lamc[:, 0:1], in0=lamc[:, 4:5], scalar=-LAM_INIT, in1=lamc[:, 3:4],
          op0=ALU.add, op1=ALU.subtract)

        class WStream:
            def __init__(self):
                self.free = list(range(NWB))
                self.pending = []
                self.loaded = {}

            def plan(self, key, loadfn):
                self.pending.append((key, loadfn))

            def pump(self):
                while self.pending and self.free:
                    key, fn = self.pending.pop(0)
                    b = self.free.pop(0)
                    self.loaded[key] = b
                    if fn is not None:
                        fn(b)

            def get(self, key):
                self.pump()
                assert key in self.loaded, ("weight tile not loaded", key, list(self.loaded), self.pending[:3])
                return self.loaded[key]

            def release(self, key):
                self.free.append(self.loaded.pop(key))
                self.pump()

        ws = WStream()

        def wload(key, parts):
            def fn(b):
                for dst_fn, src in parts:
                    S.dma("pool", w=[WR[b]], stream=f"w{b}", out=dst_fn(b), in_=src)
            ws.plan(key, fn)

        def wview(b, n, k):
            return WB[b][:, 0:n * k].rearrange("p (k n) -> p k n", k=k)

        I("act", "activation", r=[cTR], w=[cTR], out=cT_sb[:], in_=cT_sb[:], func=AF.Silu)
        I("dve", "tensor_copy", r=[cTR], w=[cabR], out=cab[:], in_=cT_sb[:])

        def plan_ada(j):
            wload(("ada", j), [(lambda b: wview(b, 512, 8), w_ada[:, j * 512:(j + 1) * 512].rearrange("(k p) n -> p k n", p=128))])

        def ada_block(j):
            key = ("ada", j)
            b = ws.get(key)
            wv = wview(b, 512, 8)
            pb = psall.next()
            for cc in range(4):
                for k in range(NK):
                    I("pe", "matmul", r=[WR[b], cabR], w=[PR[pb]], out=PS[pb][:, cc * 2:cc * 2 + 2], lhsT=wv[:, k, cc * 128:(cc + 1) * 128],
                      rhs=cab[:, k, :], start=(k == 0), stop=(k == NK - 1))
            for cc in range(4):
                n = 4 * j + cc
                I("dve", "tensor_scalar", r=[PR[pb], badaR], w=[modR], out=modT[:, n, :], in0=PS[pb][:, cc * 2:cc * 2 + 2],
                  scalar1=bada_sb[:, n:n + 1], scalar2=None, op0=ALU.add)
            ws.release(key)

        def ada_derive(subs, parts=("sb", "g")):
            coef = [0.5, 1.0, 0.5]
            for i in subs:
                for b in range(2):
                    if "sb" in parts:
                        I("dve", "scalar_tensor_tensor", r=[modR, gainsR], w=[scR], out=SC[:, i, 0, b, :],
                          in0=modT[:, (3 * i + 1) * 8:(3 * i + 2) * 8, b], scalar=1.0, in1=gains_sb[:, i, :], op0=ALU.add, op1=ALU.mult)
                        I("dve", "tensor_copy", r=[modR], w=[scR], out=SC[:, i, 1, b, :], in_=modT[:, (3 * i) * 8:(3 * i + 1) * 8, b])
                    if "g" in parts:
                        I("dve", "tensor_scalar", r=[modR], w=[scR], out=SC[:, i, 2, b, :], in0=modT[:, (3 * i + 2) * 8:(3 * i + 3) * 8, b],
                          scalar1=coef[i], scalar2=None, op0=ALU.mult)

        def tsl(tt):
            return slice(tt * 512, (tt + 1) * 512)

        def stats(srcs, split=True):
            pb = psall.next()
            n = len(srcs)
            for k, (ap, res) in enumerate(srcs):
                sq, sqR = btmp.next()
                if split and k % 2 == 1:
                    I("dve", "tensor_tensor", r=[res], w=[sqR], out=sq[:], in0=ap, in1=ap, op=ALU.mult)
                else:
                    I("act", "activation", r=[res], w=[sqR], out=sq[:], in_=ap, func=AF.Square)
                I("pe", "matmul", r=[sqR, onesR], w=[PR[pb]], out=PS[pb][:], lhsT=ones_sb[:], rhs=sq[:], start=(k == 0), stop=(k == n - 1))
            return pb

        def finish(pb, dim):
            rs, rsR = rstmp.next()
            I("dve", "tensor_scalar", r=[PR[pb]], w=[rsR], out=rs[:], in0=PS[pb][:], scalar1=1.0 / dim, scalar2=EPS, op0=ALU.mult, op1=ALU.add)
            I("act", "activation", r=[rsR], w=[rsR], out=rs[:], in_=rs[:], func=AF.Ln)
            I("act", "activation", r=[rsR], w=[rsR], out=rs[:], in_=rs[:], func=AF.Exp, scale=-0.5)
            return rs, rsR

        def rstd_bc(srcs, dim, split=True):
            return finish(stats(srcs, split), dim)

        def pipelined_norm(apply):
            hs = lambda tt: [(hT[:, k, tsl(tt)], hR[k][tt]) for k in range(NK)]
            p0 = stats(hs(0), False)
            p1 = stats(hs(1), False)
            r0 = finish(p0, D)
            apply(0, *r0)
            p2 = stats(hs(2), False)
            r1 = finish(p1, D)
            apply(1, *r1)
            p3 = stats(hs(3), False)
            r2 = finish(p2, D)
            apply(2, *r2)
            r3 = finish(p3, D)
            apply(3, *r3)

        def norm_mod(sub, b):
            def apply(tt, rs, rsR):
                for k in range(NK):
                    t1, t1R = ftmp.next()
                    I("dve", "tensor_tensor", r=[hR[k][tt], rsR], w=[t1R], out=t1[:], in0=hT[:, k, tsl(tt)], in1=rs[:], op=ALU.mult)
                    I("act", "activation", r=[t1R, scR], w=[uR[k][tt]], out=uT[:, k, tsl(tt)], in_=t1[:], func=AF.Identity,
                      bias=SC[:, sub, 1, b, k:k + 1], scale=SC[:, sub, 0, b, k:k + 1])
            pipelined_norm(apply)

        uid = [0]

        def plan_ffn(fi, plan_hook=None):
            wg, wu, wd = wgs[fi], wus[fi], wds[fi]
            uid[0] += 1
            oid = uid[0]
            order = []
            for g in range(len(FFN_GROUPS)):
                order.append(("A", g))
                if g >= 1:
                    order.append(("B", g - 1))
            order.append(("B", len(FFN_GROUPS) - 1))
            for si, (kind, g) in enumerate(order):
                f0, f1 = FFN_GROUPS[g]
                nf = f1 - f0
                if kind == "A":
                    for nm, wsrc in (("g", wg), ("u", wu)):
                        wload(("ffn", oid, nm, g), [(lambda b, nf=nf: wview(b, nf * 128, 8),
                                                     wsrc[:, f0 * 128:f1 * 128].rearrange("(k p) n -> p k n", p=128))])
                else:
                    wload(("ffn", oid, "d", g), [(lambda b, nf=nf: wview(b, 1024, nf),
                                                  wd[f0 * 128:f1 * 128, :].rearrange("(f p) n -> p f n", p=128))])
                if plan_hook is not None:
                    plan_hook(si)
            return oid, order

        def ffn(sub, b, plan, hook=None):
            oid, order = plan
            last_g = len(FFN_GROUPS) - 1
            for si, (kind, g) in enumerate(order):
                f0, f1 = FFN_GROUPS[g]
                nf = f1 - f0
                big = g % 2
                actv = BIG[big][:, 0:8192].rearrange("p (f t) -> p f t", f=4)
                if kind == "A":
                    kg, ku = ("ffn", oid, "g", g), ("ffn", oid, "u", g)
                    bg, bu = ws.get(kg), ws.get(ku)
                    wgv = wview(bg, nf * 128, 8)
                    wuv = wview(bu, nf * 128, 8)

                    def gate_up(f, tts):
                        pg = {tt: psall.next() for tt in tts}
                        for k in range(NK):
                            for tt in tts:
                                I("pe", "matmul", r=[WR[bg], uR[k][tt]], w=[PR[pg[tt]]], out=PS[pg[tt]][:], lhsT=wgv[:, k, f * 128:(f + 1) * 128],
                                  rhs=uT[:, k, tsl(tt)], start=(k == 0), stop=(k == NK - 1))
                        pu = {tt: psall.next() for tt in tts}
                        for k in range(NK):
                            for tt in tts:
                                I("pe", "matmul", r=[WR[bu], uR[k][tt]], w=[PR[pu[tt]]], out=PS[pu[tt]][:], lhsT=wuv[:, k, f * 128:(f + 1) * 128],
                                  rhs=uT[:, k, tsl(tt)], start=(k == 0), stop=(k == NK - 1))
                        sgs = {}
                        for tt in tts:
                            sg, sgR = ftmp.next()
                            sgs[tt] = (sg, sgR)
                            I("act", "activation", r=[PR[pg[tt]]], w=[sgR], out=sg[:], in_=PS[pg[tt]][:], func=AF.Silu)
                        for tt in tts:
                            sg, sgR = sgs[tt]
                            I("dve", "tensor_tensor", r=[sgR, PR[pu[tt]]], w=bigR[big], out=actv[:, f, tsl(tt)], in0=sg[:], in1=PS[pu[tt]][:], op=ALU.mult)

                    if g == 0:
                        for tt in range(NTT):
                            for f in range(nf):
                                gate_up(f, [tt])
                    else:
                        for f in range(nf):
                            gate_up(f, list(range(NTT)))
                    ws.release(kg)
                    ws.release(ku)
                else:
                    kd = ("ffn", oid, "d", g)
                    bd = ws.get(kd)
                    wdv = wview(bd, 1024, nf)

                    def down(m, tts):
                        po = {tt: psall.next() for tt in tts}
                        for f in range(nf):
                            for tt in tts:
                                I("pe", "matmul", r=[WR[bd]] + bigR[big], w=[PR[po[tt]]], out=PS[po[tt]][:], lhsT=wdv[:, f, m * 128:(m + 1) * 128],
                                  rhs=actv[:, f, tsl(tt)], start=(f == 0), stop=(f == nf - 1))
                        for tt in tts:
                            I("dve", "scalar_tensor_tensor", r=[PR[po[tt]], scR, hR[m][tt]], w=[hR[m][tt]], out=hT[:, m, tsl(tt)], in0=PS[po[tt]][:],
                              scalar=SC[:, sub, 2, b, m:m + 1], in1=hT[:, m, tsl(tt)], op0=ALU.mult, op1=ALU.add)

                    if g == last_g:
                        for tt in range(NTT):
                            for m in range(NK):
                                down(m, [tt])
                    else:
                        for m in range(NK):
                            down(m, list(range(NTT)))
                    ws.release(kd)
                if hook is not None:
                    hook(si)

        def load_x(b):
            for tt in range(NTT):
                grp = [S.dma("sp", w=[hR[k][tt]], stream=f"x{tt}", out=hT[:, k, tsl(tt)], in_=xT[b, k * 128:(k + 1) * 128, tsl(tt)]) for k in range(NK)]
                for o in grp:
                    o.dval = grp[-1].dval

        outs = []

        def final_store(b):
            def apply(tt, rs, rsR):
                for k in range(NK):
                    I("dve", "scalar_tensor_tensor", r=[hR[k][tt], gainsR, rsR], w=[hR[k][tt]], out=hT[:, k, tsl(tt)], in0=hT[:, k, tsl(tt)],
                      scalar=gains_sb[:, 3, k:k + 1], in1=rs[:], op0=ALU.mult, op1=ALU.mult)
                grp = [S.dma("sp", r=[hR[k][tt]], stream=f"o{tt}", out=outT[b, k * 128:(k + 1) * 128, tsl(tt)], in_=hT[:, k, tsl(tt)]) for k in range(NK)]
                for o in grp:
                    o.dval = grp[-1].dval
                outs.extend(grp)
            pipelined_norm(apply)

        Ares, Blo, Bhi = bigR[0][0], bigR[1][0], bigR[1][1]
        qv = BIG[0][:, 0:4096].rearrange("p (c t) -> p c t", c=2)
        kv = BIG[0][:, 4096:8192].rearrange("p (c t) -> p c t", c=2)
        vaug = BIG[1][:, 0:4128].rearrange("p (j c d) -> p j c d", j=16, c=2)
        tab = BIG[1][:, 4160:8256].rearrange("p (s t) -> p s t", s=2)
        ps7b = PS[7][:].bitcast(BF16)
        PI_C = 3.1415925

        def plan_mixer():
            uid[0] += 1
            oid = uid[0]
            for a in range(2):
                wload(("qk", oid, a), [
                    (lambda b: wview(b, 512, 8)[:, :, 0:256], w_in[:, a * 256:(a + 1) * 256].rearrange("(k p) n -> p k n", p=128)),
                    (lambda b: wview(b, 512, 8)[:, :, 256:512], w_in[:, 512 + a * 256:512 + (a + 1) * 256].rearrange("(k p) n -> p k n", p=128))])
                wload(("v", oid, a), [(lambda b: wview(b, 256, 8), w_in[:, 1024 + a * 256:1024 + (a + 1) * 256].rearrange("(k p) n -> p k n", p=128))])
                ws.plan(("mixT", oid, a), None)
                wload(("wo", oid, a), [(lambda b: wview(b, 1024, 2), w_out[a * 256:(a + 1) * 256, :].rearrange("(f p) n -> p f n", p=128))])
            wload(("lat1", oid), [(lambda b: wview(b, 384, 8), w_in[:, 1536:1920].rearrange("(k p) n -> p k n", p=128))])
            wload(("lat2", oid), [(lambda b: wview(b, 288, 8), w_in[:, 1920:2208].rearrange("(k p) n -> p k n", p=128))])
            wload(("wmla", oid), [
                (lambda b: WB[b][:, 0:1152].rearrange("p (k n) -> p k n", k=3), w_uq.rearrange("(k p) n -> p k n", p=128)),
                (lambda b: WB[b][:, 1152:2688].rearrange("p (k n) -> p k n", k=2), w_ukv.rearrange("(k p) n -> p k n", p=128))])
            ws.plan(("omT", oid, 0), None)
            ws.plan(("omT", oid, 1), None)
            wload(("wo", oid, "mla"), [(lambda b: wview(b, 1024, 4), w_out[512:1024, :].rearrange("(f p) n -> p f n", p=128))])
            return oid

        def tables(b, col, rows, tts=range(NTT)):
            for tt in tts:
                S.dma("sp", w=[posiR], stream="pos", out=posi[:], in_=pos[b:b + 1, tsl(tt)].broadcast_to([128, 512]))
                posf, posfR = ftmp.next()
                I("dve", "tensor_copy", r=[posiR], w=[posfR], out=posf[:], in_=posi[:])
                ang, angR = ftmp.next()
                I("dve", "tensor_scalar", r=[posfR, invfR], w=[angR], out=ang[:], in0=posf[:], scalar1=invf_sb[:, col:col + 1], scalar2=None, op0=ALU.mult)
                for si, shift in ((1, 0.0), (0, 0.5 * math.pi)):
                    if shift != 0.0:
                        a2, a2R = ftmp.next()
                        I("dve", "tensor_scalar", r=[angR], w=[a2R], out=a2[:], in0=ang[:], scalar1=shift, scalar2=None, op0=ALU.add)
                    else:
                        a2, a2R = ang, angR
                    I("dve", "tensor_scalar", r=[a2R], w=[itmpR], out=itmp[:], in0=a2[:], scalar1=1.0 / TWO_PI, scalar2=None, op0=ALU.mult)
                    kf, kfR = ftmp.next()
                    I("dve", "tensor_copy", r=[itmpR], w=[kfR], out=kf[:], in_=itmp[:])
                    y, yR = ftmp.next()
                    I("dve", "scalar_tensor_tensor", r=[kfR, a2R], w=[yR], out=y[:], in0=kf[:], scalar=-C1, in1=a2[:], op0=ALU.mult, op1=ALU.add)
                    I("dve", "scalar_tensor_tensor", r=[kfR, yR], w=[yR], out=y[:], in0=kf[:], scalar=-C2, in1=y[:], op0=ALU.mult, op1=ALU.add)
                    I("dve", "tensor_scalar", r=[yR], w=[yR], out=y[:], in0=y[:], scalar1=-PI_C, scalar2=PI_C, op0=ALU.max, op1=ALU.min)
                    I("act", "activation", r=[yR], w=[Bhi], out=tab[rows, si, tsl(tt)], in_=y[rows, :], func=AF.Sin)

        pend = []

        def defer(fn):
            if pend:
                pend.pop(0)()
            pend.append(fn)

        def flush():
            while pend:
                pend.pop(0)()

        def rope(psrc, srcR, rows, rmat, dst, dstR, tt, deferred=True):
            qb, qbR = btmp.next()
            I("act", "activation", r=[srcR], w=[qbR], out=qb[rows, :], in_=PS[psrc][rows, :], func=AF.Identity)

            def partB():
                pr = psall.next()
                M = rows.stop
                I("pe", "matmul", r=[qbR, cmR], w=[PR[pr]], out=PS[pr][0:M, :], lhsT=rmat[rows, 0:M], rhs=qb[rows, :], start=True, stop=True)
                t1, t1R = ftmp.next()
                I("dve", "tensor_tensor", r=[srcR, Bhi], w=[t1R], out=t1[rows, :], in0=tab[rows, 0, tsl(tt)], in1=PS[psrc][rows, :], op=ALU.mult)
                t2, t2R = ftmp.next()
                I("dve", "tensor_tensor", r=[PR[pr], Bhi], w=[t2R], out=t2[rows, :], in0=tab[rows, 1, tsl(tt)], in1=PS[pr][rows, :], op=ALU.mult)
                I("dve", "tensor_tensor", r=[t1R, t2R], w=[dstR], out=dst, in0=t1[rows, :], in1=t2[rows, :], op=ALU.add)
            if deferred:
                defer(partB)
            else:
                partB()

        evq = []

        def tick(n):
            for _ in range(min(n, len(evq))):
                evq.pop(0)()

        stb = Rot([4, 5, 6, 7])
        psb = {i: PS[i][:].bitcast(BF16) for i in (4, 5, 6, 7)}

        stpair = Rot([4, 6])

        def attention_pass(groups, scale, evac, share):
            ptp.wide = not share
            nh = 2 if share else 1
            bph = 4 // nh
            W = bph * 128
            L = 3 if share else 1
            steps = []
            for g in range(len(groups)):
                for qc in range(4):
                    for j in range(4 * qc + 4):
                        r = max(0, j - 4 * qc)
                        for h in range(nh):
                            blk0 = max(bph * h, r)
                            nb = bph * (h + 1) - blk0
                            if nb > 0:
                                steps.append((g, qc, j, h, blk0, nb))
            lasts = {}
            for idx, (g, qc, j, h, blk0, nb) in enumerate(steps):
                lasts[(g, qc)] = idx
            state = {}

            def qk(idx):
                g, qc, j, h, blk0, nb = steps[idx]
                r = j - 4 * qc
                q0 = qc * 512 + blk0 * 128
                nq = nb * 128
                if share:
                    st = stb.next()
                    res = [PR[st]]
                    base = st * 512
                else:
                    st = stpair.next()
                    res = [PR[st], PR[st + 1]]
                    base = st * 512
                diag = (r >= 0 and blk0 == r)
                for m, (c, rs_) in enumerate(groups[g]):
                    I("pe", "matmul", r=[Ares], w=res, out=PSALL[:, base + m * W:base + m * W + nq], lhsT=kv[rs_, c, j * 128:(j + 1) * 128],
                      rhs=qv[rs_, c, q0:q0 + nq], start=(m == 0 or not share), stop=(not diag), skip_group_check=True)
                if diag:
                    for m in range(2):
                        I("pe", "matmul", r=[cmR], w=res, out=PSALL[:, base + m * W:base + m * W + 128], lhsT=ident, rhs=maskT,
                          start=False, stop=True, skip_group_check=True)
                pt, ptR = ptp.next()
                I("act", "activation", r=res, w=[ptR], out=pt[:, 0:2 * W].rearrange("p (m n) -> p m n", m=2)[:, :, 0:nq],
                  in_=PSALL[:, base:base + 2 * W].rearrange("p (m n) -> p m n", m=2)[:, :, 0:nq], func=AF.Exp, scale=scale)
                state[idx] = (pt, ptR)

            def pv(idx):
                g, qc, j, h, blk0, nb = steps[idx]
                pt, ptR = state.pop(idx)
                for m, (c, rs_) in enumerate(groups[g]):
                    for bi in range(nb):
                        i = blk0 + bi
                        bank, off = m * 2 + i // 2, (i % 2) * 129
                        I("pe", "matmul", r=[ptR, Blo], w=[PR[bank]], out=PS[bank][:, off:off + 129], lhsT=pt[:, m * W + bi * 128:m * W + (bi + 1) * 128],
                          rhs=vaug[:, j, c, :], start=(j == 0 and i % 2 == 0), stop=(j == 4 * qc + i), skip_group_check=True)
                if lasts[(g, qc)] == idx:
                    evac(g, qc)

            n = len(steps)
            for i in range(min(L, n)):
                qk(i)
            for i in range(n):
                if i + L < n:
                    qk(i + L)
                pv(i)
                tick(1)
            tick(len(evq))

        accv = accsb[:].rearrange("p j (i d) -> p (j i) d", i=2)

        def transpose_out(srcs, dstap, dstR):
            tb_ = stb.next()
            for i, (ap, res) in enumerate(srcs):
                I("pe", "transpose", r=[res, cmR], w=[PR[tb_]], out=psb[tb_][:, i * 128:(i + 1) * 128], in_=ap, identity=ident)
            I("dve", "tensor_copy", r=[PR[tb_]], w=[dstR], out=dstap, in_=psb[tb_][:, 0:512])

        def out_proj(b, bo, nch, srcv, srcR):
            wov = wview(bo, 1024, nch)
            for m in range(NK):
                po = [psall.next() for _ in range(NTT)]
                for j in range(nch):
                    for tt in range(NTT):
                        I("pe", "matmul", r=[WR[bo], srcR], w=[PR[po[tt]]], out=PS[po[tt]][:], lhsT=wov[:, j, m * 128:(m + 1) * 128],
                          rhs=srcv[:, j, tsl(tt)], start=(j == 0), stop=(j == nch - 1))
                for tt in range(NTT):
                    I("dve", "scalar_tensor_tensor", r=[PR[po[tt]], scR, hR[m][tt]], w=[hR[m][tt]], out=hT[:, m, tsl(tt)], in0=PS[po[tt]][:],
                      scalar=SC[:, 1, 2, b, m:m + 1], in1=hT[:, m, tsl(tt)], op0=ALU.mult, op1=ALU.add)

        def mixer(b, oid):
            full = slice(0, 128)
            for a in range(2):
                bqk, bv = ws.get(("qk", oid, a)), ws.get(("v", oid, a))
                wqk = wview(bqk, 512, 8)
                wv = wview(bv, 256, 8)
                I("dve", "memset", w=[Blo], ap=vaug[:, :, :, 128:129], constant=1.0)
                if MIXSTOP[0] == 21:
                    return
                for tt in range(NTT):
                    if a == 0:
                        tables(b, 0, full, [tt])
                    for c in range(2):
                        for which, dstv in ((0, qv), (1, kv)):
                            pq = psall.next()
                            for k in range(NK):
                                I("pe", "matmul", r=[WR[bqk], uR[k][tt]], w=[PR[pq]], out=PS[pq][:], lhsT=wqk[:, k, which * 256 + c * 128:which * 256 + (c + 1) * 128],
                                  rhs=uT[:, k, tsl(tt)], start=(k == 0), stop=(k == NK - 1))
                            rope(pq, PR[pq], full, rdiff, dstv[:, c, tsl(tt)], Ares, tt)
                    for tb in range(4):
                        blk = tt * 4 + tb
                        pv = psall.next()
                        for k in range(NK):
                            I("pe", "matmul", r=[WR[bv], uR[k][tt]], w=[PR[pv]], out=PS[pv][:, 0:256], lhsT=uT[:, k, blk * 128:(blk + 1) * 128],
                              rhs=wv[:, k, :], start=(k == 0), stop=(k == NK - 1))
                        I("act", "activation", r=[PR[pv]], w=[Blo], out=vaug[:, blk, :, 0:128], in_=PS[pv][:, 0:256].rearrange("p (c d) -> p c d", c=2), func=AF.Identity)
                flush()
                ws.release(("qk", oid, a))
                ws.release(("v", oid, a))
                bm = ws.get(("mixT", oid, a))
                mixv = wview(bm, 2048, 2)

                def evac_diff(c, qc, mixv=mixv, bm=bm):
                    tick(len(evq))
                    for j in range(4):
                        I("dve", "tensor_copy", r=[PR[j]], w=[accsbR], out=accsb[:, j, :], in_=PS[j][:, 0:258])
                    I("dve", "reciprocal", r=[accsbR], w=[smallR], out=small[:, 0:8], in_=accv[:, :, 128])
                    I("dve", "tensor_scalar", r=[smallR, lamcR], w=[smallR], out=small[:, 4:8], in0=small[:, 4:8], scalar1=lamc[:, 0:1], scalar2=None, op0=ALU.mult)
                    I("dve", "memset", w=[smallR], ap=small[:, 8:12], constant=0.0)

                    def blk(i):
                        t, tR = osb.next()
                        I("dve", "tensor_scalar", r=[accsbR, smallR], w=[tR], out=t[:], in0=accv[:, 4 + i, 0:128], scalar1=small[:, 4 + i:5 + i], scalar2=None, op0=ALU.mult)
                        I("dve", "scalar_tensor_tensor", r=[accsbR, smallR, tR], w=[accsbR], out=accv[:, i, 0:128], in0=accv[:, i, 0:128],
                          scalar=small[:, i:i + 1], in1=t[:], op0=ALU.mult, op1=ALU.add)
                        I("dve", "tensor_tensor", r=[accsbR], w=[junkR], out=junk[:], in0=accv[:, i, 0:128], in1=accv[:, i, 0:128], op=ALU.mult)
                        I("dve", "reduce_sum", r=[junkR], w=[smallR], out=small[:, 8 + i:9 + i], in_=junk[:], axis=AX)
                    for i in range(4):
                        evq.append(lambda i=i: blk(i))

                    def rstd_a():
                        I("dve", "tensor_scalar", r=[smallR], w=[smallR], out=small[:, 8:12], in0=small[:, 8:12], scalar1=1.0 / 128, scalar2=EPS, op0=ALU.mult, op1=ALU.add)

                    def rstd_b():
                        I("act", "activation", r=[smallR], w=[smallR], out=small[:, 8:12], in_=small[:, 8:12], func=AF.Ln)
                        I("act", "activation", r=[smallR], w=[smallR], out=small[:, 8:12], in_=small[:, 8:12], func=AF.Exp, scale=-0.5)
                    evq.append(rstd_a)
                    evq.append(lambda: None)
                    evq.append(rstd_b)
                    srcs = []

                    def scale_out():
                        for i in range(4):
                            ob, obR = obf.next()
                            I("dve", "tensor_scalar", r=[accsbR, smallR], w=[obR], out=ob[:], in0=accv[:, i, 0:128], scalar1=small[:, 8 + i:9 + i], scalar2=None, op0=ALU.mult)
                            srcs.append((ob[:], obR))
                    evq.append(scale_out)
                    evq.append(lambda: transpose_out(srcs, mixv[:, c, tsl(qc)], WR[bm]))

                attention_pass([[(c, slice(0, 64)), (c, slice(64, 128))] for c in range(2)], 0.125, evac_diff, False)
                if MIXSTOP[0] == 3:
                    return
                bo = ws.get(("wo", oid, a))
                wov = wview(bo, 1024, 2)
                I("dve", "tensor_scalar", r=[WR[bo], gsmR], w=[WR[bo]], out=wov[:], in0=wov[:], scalar1=gsm_sb[:, 0:1], scalar2=1.0 - LAM_INIT, op0=ALU.mult, op1=ALU.mult)
                out_proj(b, bo, 2, mixv, WR[bm])
                ws.release(("mixT", oid, a))
                ws.release(("wo", oid, a))

            r96 = slice(64, 96)
            b1, b2, bw = ws.get(("lat1", oid)), ws.get(("lat2", oid)), ws.get(("wmla", oid))
            l1 = wview(b1, 384, 8)
            l2 = wview(b2, 288, 8)
            wuq = WB[bw][:, 0:1152].rearrange("p (k n) -> p k n", k=3)
            wukv = WB[bw][:, 1152:2688].rearrange("p (k n) -> p k n", k=2)
            wukv4 = WB[bw][:, 1152:2688].rearrange("p (k h n) -> p k h n", k=2, h=4)
            for tt in range(NTT):
                tables(b, 1, r96, [tt])
                pc = [psall.next() for _ in range(3)]
                for j in range(3):
                    for k in range(NK):
                        I("pe", "matmul", r=[WR[b1], uR[k][tt]], w=[PR[pc[j]]], out=PS[pc[j]][:], lhsT=l1[:, k, j * 128:(j + 1) * 128], rhs=uT[:, k, tsl(tt)],
                          start=(k == 0), stop=(k == NK - 1))
                pk = [psall.next() for _ in range(2)]
                for j in range(2):
                    for k in range(NK):
                        I("pe", "matmul", r=[WR[b2], uR[k][tt]], w=[PR[pk[j]]], out=PS[pk[j]][:], lhsT=l2[:, k, j * 128:(j + 1) * 128], rhs=uT[:, k, tsl(tt)],
                          start=(k == 0), stop=(k == NK - 1))
                pkr = psall.next()
                for k in range(NK):
                    I("pe", "matmul", r=[WR[b2], uR[k][tt]], w=[PR[pkr]], out=PS[pkr][0:96, :], lhsT=l2[:, k, 192:288], rhs=uT[:, k, tsl(tt)],
                      start=(k == 0), stop=(k == NK - 1))
                rs, rsR = rstd_bc([(PS[pc[j]][:], PR[pc[j]]) for j in range(3)], 384, split=False)
                for j in range(3):
                    I("dve", "scalar_tensor_tensor", r=[PR[pc[j]], gsmR, rsR], w=[uR[j][tt]], out=uT[:, j, tsl(tt)], in0=PS[pc[j]][:], scalar=gsm_sb[:, 1 + j:2 + j],
                      in1=rs[:], op0=ALU.mult, op1=ALU.mult)
                rs, rsR = rstd_bc([(PS[pk[j]][:], PR[pk[j]]) for j in range(2)], 256, split=False)
                for j in range(2):
                    I("dve", "scalar_tensor_tensor", r=[PR[pk[j]], gsmR, rsR], w=[uR[3 + j][tt]], out=uT[:, 3 + j, tsl(tt)], in0=PS[pk[j]][:], scalar=gsm_sb[:, 4 + j:5 + j],
                      in1=rs[:], op0=ALU.mult, op1=ALU.mult)
                rope(pkr, PR[pkr], r96, rmla, uT[r96, 5, tsl(tt)], uR[5][tt], tt, deferred=False)
            ws.release(("lat1", oid))
            ws.release(("lat2", oid))

            oms = []
            for a in range(2):
                I("dve", "memset", w=[Blo], ap=vaug[:, :, :, 128:129], constant=1.0)
                for tt in range(NTT):
                    for c in range(2):
                        hq = 2 * a + c
                        pq = psall.next()
                        for j in range(3):
                            I("pe", "matmul", r=[WR[bw], uR[j][tt]], w=[PR[pq]], out=PS[pq][0:96, :], lhsT=wuq[:, j, hq * 96:(hq + 1) * 96], rhs=uT[:, j, tsl(tt)],
                              start=(j == 0), stop=(j == 2))
                        I("act", "activation", r=[PR[pq]], w=[Ares], out=qv[0:64, c, tsl(tt)], in_=PS[pq][0:64, :], func=AF.Identity)
                        rope(pq, PR[pq], r96, rmla, qv[r96, c, tsl(tt)], Ares, tt)
                        pkk = psall.next()
                        for j in range(2):
                            I("pe", "matmul", r=[WR[bw], uR[3 + j][tt]], w=[PR[pkk]], out=PS[pkk][0:64, :], lhsT=wukv[:, j, hq * 192:hq * 192 + 64], rhs=uT[:, 3 + j, tsl(tt)],
                              start=(j == 0), stop=(j == 1))
                        I("act", "activation", r=[PR[pkk]], w=[Ares], out=kv[0:64, c, tsl(tt)], in_=PS[pkk][0:64, :], func=AF.Identity)
                        I("dve", "tensor_copy", r=[uR[5][tt]], w=[Ares], out=kv[r96, c, tsl(tt)], in_=uT[r96, 5, tsl(tt)])
                    flush()
                    for tb in range(4):
                        blk = tt * 4 + tb
                        pv = psall.next()
                        for j in range(2):
                            I("pe", "matmul", r=[WR[bw], uR[3 + j][tt]], w=[PR[pv]], out=PS[pv][:, 0:256].rearrange("p (c d) -> p c d", c=2),
                              lhsT=uT[:, 3 + j, blk * 128:(blk + 1) * 128], rhs=wukv4[:, j, 2 * a:2 * a + 2, 64:192], start=(j == 0), stop=(j == 1))
                        I("act", "activation", r=[PR[pv]], w=[Blo], out=vaug[:, blk, :, 0:128], in_=PS[pv][:, 0:256].rearrange("p (c d) -> p c d", c=2), func=AF.Identity)
                flush()
                bm = ws.get(("omT", oid, a))
                omv = wview(bm, 2048, 2)
                oms.append((bm, omv))

                def evac_mla(g, qc, bm=bm, omv=omv):
                    tick(len(evq))
                    for j in range(4):
                        I("dve", "tensor_copy", r=[PR[j]], w=[accsbR], out=accsb[:, j, :], in_=PS[j][:, 0:258])
                    I("dve", "reciprocal", r=[accsbR], w=[smallR], out=small[:, 0:8], in_=accv[:, :, 128])
                    for m in range(2):
                        srcs = []

                        def scale_out(m=m, srcs=srcs):
                            for i in range(4):
                                ob, obR = obf.next()
                                I("dve", "tensor_scalar", r=[accsbR, smallR], w=[obR], out=ob[:], in0=accv[:, m * 4 + i, 0:128], scalar1=small[:, m * 4 + i:m * 4 + i + 1],
                                  scalar2=None, op0=ALU.mult)
                                srcs.append((ob[:], obR))
                        evq.append(scale_out)
                        evq.append(lambda m=m, srcs=srcs: transpose_out(srcs, omv[:, m, tsl(qc)], WR[bm]))

                attention_pass([[(0, slice(0, 96)), (1, slice(0, 96))]], 96.0 ** -0.5, evac_mla, MIXSTOP[0] != 7)
                if MIXSTOP[0] in (6, 7):
                    return
            ws.release(("wmla", oid))
            for tt in range(NTT):
                srcs = [(oms[h // 2][1][:, h % 2, tsl(tt)], WR[oms[h // 2][0]]) for h in range(4)]
                rs, rsR = rstd_bc(srcs, 512)
                for h in range(4):
                    bm, omv = oms[h // 2]
                    I("dve", "tensor_tensor", r=[WR[bm], rsR], w=[WR[bm]], out=omv[:, h % 2, tsl(tt)], in0=omv[:, h % 2, tsl(tt)], in1=rs[:], op=ALU.mult)
            bo = ws.get(("wo", oid, "mla"))
            wov = wview(bo, 1024, 4)
            for j in range(4):
                I("dve", "tensor_scalar", r=[WR[bo], gsmR], w=[WR[bo]], out=wov[:, j, :], in0=wov[:, j, :], scalar1=gsm_sb[:, 6 + j:7 + j], scalar2=None, op0=ALU.mult)
            wov4 = wview(bo, 1024, 4)
            for tt in range(NTT):
                for m in range(NK):
                    po = psall.next()
                    for h in range(4):
                        bm, omv = oms[h // 2]
                        I("pe", "matmul", r=[WR[bo], WR[bm]], w=[PR[po]], out=PS[po][:], lhsT=wov4[:, h, m * 128:(m + 1) * 128],
                          rhs=omv[:, h % 2, tsl(tt)], start=(h == 0), stop=(h == 3))
                    I("dve", "scalar_tensor_tensor", r=[PR[po], scR, hR[m][tt]], w=[hR[m][tt]], out=hT[:, m, tsl(tt)], in0=PS[po][:],
                      scalar=SC[:, 1, 2, b, m:m + 1], in1=hT[:, m, tsl(tt)], op0=ALU.mult, op1=ALU.add)
            ws.release(("omT", oid, 0))
            ws.release(("omT", oid, 1))
            ws.release(("wo", oid, "mla"))

        NADA0 = 4 if "ffn1" in stages else 18
        hook_blocks = {0: [4, 5]}
        for si in range(1, 12):
            hook_blocks[si] = [5 + si]
        for j in range(NADA0):
            plan_ada(j)
        plans = {}
        for s in range(nseq):
            if "ffn1" in stages:
                hookp = (lambda si: [plan_ada(j) for j in hook_blocks[si]]) if s == 0 else None
                plans[(s, 0)] = plan_ffn(0, hookp)
                if s == 0:
                    plan_ada(17)
            if "mix" in stages:
                plans[(s, "mix")] = plan_mixer()
            if "ffn2" in stages:
                plans[(s, 1)] = plan_ffn(1)
        load_x(0)
        for j in range(NADA0):
            ada_block(j)
        if NADA0 == 18:
            ada_derive([0, 1, 2])
        else:
            ada_derive([0], ("sb",))
        for s in range(nseq):
            if s > 0:
                load_x(s)
            if "ffn1" in stages:
                norm_mod(0, s)
                if s == 0:
                    def hook(si):
                        for j in hook_blocks[si]:
                            ada_block(j)
                        if si == 0:
                            ada_derive([0], ("g",))
                    ffn(0, s, plans[(s, 0)], hook=hook)
                    ada_block(17)
                    ada_derive([1, 2])
                else:
                    ffn(0, s, plans[(s, 0)])
            if "mix" in stages:
                norm_mod(1, s)
                mixer(s, plans[(s, "mix")])
            if "ffn2" in stages:
                norm_mod(2, s)
                ffn(2, s, plans[(s, 1)])
            final_store(s)
        if debug:
            outs.append(S.dma("sp", r=[scR], stream="dbg", out=dbg, in_=SC[:].rearrange("p a b c d -> p (a b c d)")))
        S.emit(final_waits=outs)
    return nc


def _host_inputs(inputs):
    f = lambda a: np.ascontiguousarray(np.asarray(a, dtype=np.float32))
    x = np.asarray(inputs["x"], dtype=np.float32)
    c = np.asarray(inputs["c"], dtype=np.float32)
    positions = np.asarray(inputs["positions"]).astype(np.int32)
    col = lambda v, n: np.ascontiguousarray(np.asarray(v, np.float32).reshape(n, 128).T)
    gains = np.stack([col(inputs["ffn1_norm"][0], 8), col(inputs["mix_norm"][0], 8), col(inputs["ffn2_norm"][0], 8),
                      col(inputs["final_norm"], 8)], axis=1)
    gsmall = np.concatenate([col(inputs["diff_subln"][0], 1), col(inputs["mla_q_norm"][0], 3), col(inputs["mla_kv_norm"][0], 2),
                             col(inputs["mla_out_norm"][0], 4)], axis=1)
    lamv = np.stack([inputs["diff_lambda_q1"][0], inputs["diff_lambda_k1"][0], inputs["diff_lambda_q2"][0], inputs["diff_lambda_k2"][0]]).astype(np.float32)
    ident = np.eye(128, dtype=np.float32)
    rdiff = np.zeros((128, 128), np.float32)
    for j in range(128):
        if (j % 64) < 32:
            rdiff[j + 32, j] = -1.0
        else:
            rdiff[j - 32, j] = 1.0
    rmla = np.zeros((128, 128), np.float32)
    for j in range(64, 96):
        if (j - 64) < 16:
            rmla[j + 16, j] = -1.0
        else:
            rmla[j - 16, j] = 1.0
    maskT = np.where(np.arange(128)[None, :] >= np.arange(128)[:, None], 0.0, -30000.0).astype(np.float32)
    cmats = np.stack([ident, rdiff, rmla, maskT])
    invf = np.zeros((128, 2), np.float32)
    p = np.arange(128)
    invf[:, 0] = (10000.0 ** (-(2.0 * (p % 32)) / 64.0)).astype(np.float32)
    invf[:, 1] = (10000.0 ** (-(2.0 * (p % 16)) / 32.0)).astype(np.float32)
    shared = dict(
        w_ada=f(inputs["w_ada"][0]), b_adaT=col(inputs["b_ada"][0], 72), gains=np.ascontiguousarray(gains),
        wg1=f(inputs["ffn1_w_gate"][0]), wu1=f(inputs["ffn1_w_up"][0]), wd1=f(inputs["ffn1_w_down"][0]),
        wg2=f(inputs["ffn2_w_gate"][0]), wu2=f(inputs["ffn2_w_up"][0]), wd2=f(inputs["ffn2_w_down"][0]),
        w_in=f(inputs["w_in"][0]), w_out=f(inputs["w_out"][0]), w_uq=f(inputs["mla_w_uq"][0]), w_ukv=f(inputs["mla_w_ukv"][0]),
        lamv=np.ascontiguousarray(lamv), gsmall=np.ascontiguousarray(gsmall), cmats=cmats, invf=invf)
    maps = []
    for i in range(8):
        bs = slice(2 * i, 2 * i + 2)
        m = dict(shared)
        m["xT"] = np.ascontiguousarray(x[bs].transpose(0, 2, 1))
        m["cT"] = np.ascontiguousarray(c[bs].reshape(2, 8, 128).transpose(2, 1, 0))
        m["pos"] = np.ascontiguousarray(positions[bs])
        maps.append(m)
    return maps


def kernel(**inputs):
    maps = _host_inputs(inputs)
    nc = build()
    res = run_bass_kernel_spmd(nc, maps, core_ids=list(range(8)))
    out = np.empty((16, T, D), np.float32)
    for i in range(8):
        out[2 * i:2 * i + 2] = res.results[i]["outT"].transpose(0, 2, 1)
    return out
```

```python
import math
import contextlib
import numpy as np
import concourse.bass as bass
import concourse.mybir as mybir
from concourse.bass_utils import run_bass_kernel_spmd

F32 = mybir.dt.float32
BF16 = mybir.dt.bfloat16
I32 = mybir.dt.int32
AF = mybir.ActivationFunctionType
ALU = mybir.AluOpType

ENGS = ("pe", "act", "dve", "pool", "sp")
STRICT_SAME_ENGINE = True
FOLD_WAITS = True


class Res:
    __slots__ = ("name", "w", "rs", "excl")

    def __init__(self, name, excl=False):
        self.name = name
        self.w = None
        self.rs = []
        self.excl = excl


class Op:
    __slots__ = ("eng", "fn", "deps", "inc", "tick", "dma", "dsem", "dval")

    def __init__(self, eng, fn, dma):
        self.eng = eng
        self.fn = fn
        self.deps = []
        self.inc = False
        self.tick = 0
        self.dma = dma
        self.dsem = None
        self.dval = 0


class Sched:
    def __init__(self, nc):
        self.nc = nc
        self.ops = {e: [] for e in ENGS}
        self.streams = {}

    def _dep(self, op, p, kind):
        if p is None or p is op:
            return
        if not p.dma and not op.dma and p.eng == op.eng:
            if op.eng == "pe" or (kind != "raw" and not STRICT_SAME_ENGINE):
                return
        op.deps.append(p)
        if not p.dma:
            p.inc = True

    def op(self, eng, name, r=(), w=(), dma=False, stream=None, **kw):
        o = Op(eng, (name, kw), dma)
        def last_readers(rs):
            last = {}
            out = []
            for q in rs:
                if q.dma:
                    out.append(q)
                else:
                    last[q.eng] = q
            return out + list(last.values())

        for x in r:
            self._dep(o, x.w, "raw")
            if x.excl:
                for q in last_readers(x.rs):
                    if q.eng != eng or q.dma:
                        self._dep(o, q, "raw")
        for x in w:
            self._dep(o, x.w, "waw")
            for q in last_readers(x.rs):
                self._dep(o, q, "war")
        for x in r:
            x.rs.append(o)
        for x in w:
            x.w = o
            x.rs = []
        if dma:
            st = self.streams.setdefault(stream, [0])
            st[0] += 16
            o.dsem = stream
            o.dval = st[0]
        self.ops[eng].append(o)
        return o

    def dma(self, eng, r=(), w=(), stream=None, **kw):
        return self.op(eng, "dma_start", r, w, dma=True, stream=stream, **kw)

    def emit(self, final_waits=()):
        nc = self.nc
        for e in ENGS:
            t = 0
            for o in self.ops[e]:
                if not o.dma and o.inc:
                    t += 1
                    o.tick = t
        with contextlib.ExitStack() as es:
            esem = {e: es.enter_context(nc.semaphore("s_" + e)) for e in ENGS if e != "sp"}
            dsem = {name: es.enter_context(nc.semaphore("d_" + str(name))) for name in self.streams}
            block = es.enter_context(nc.Block())

            def run(ename, eng):
                seen = {}
                for o in self.ops[ename]:
                    need = {}
                    for p in o.deps:
                        if p.dma:
                            key, val, sem = ("d", p.dsem), p.dval, dsem[p.dsem]
                        else:
                            key, val, sem = ("e", p.eng), p.tick, esem[p.eng]
                        if seen.get(key, 0) < val and need.get(key, (0, None))[0] < val:
                            need[key] = (val, sem)
                    waits = list(need.items())
                    fold = None
                    if FOLD_WAITS and waits and not o.dma:
                        fold = waits.pop()
                    for key, (val, sem) in waits:
                        eng.wait_ge(sem, val)
                        seen[key] = val
                    ins = getattr(eng, o.fn[0])(**o.fn[1])
                    if fold is not None:
                        key, (val, sem) = fold
                        ins._wait_ge(sem, val)
                        seen[key] = val
                    if o.dma:
                        ins.then_inc(dsem[o.dsem], 16)
                    elif o.inc:
                        ins.then_inc(esem[ename], 1)
                if ename == "sp":
                    for p in final_waits:
                        eng.wait_ge(dsem[p.dsem], p.dval)

            block.tensor(lambda eng: run("pe", eng))
            block.scalar(lambda eng: run("act", eng))
            block.vector(lambda eng: run("dve", eng))
            block.gpsimd(lambda eng: run("pool", eng))
            block.sync(lambda eng: run("sp", eng))


D = 1024
T = 2048
FF = 2816
NK = 8
NTT = 4
EPS = 1e-6
TWO_PI = 2.0 * math.pi
C1 = 6.28125
C2 = TWO_PI - C1
FFN_GROUPS = [(0, 4), (4, 8), (8, 12), (12, 16), (16, 20), (20, 22)]
LAM_INIT = 0.8 - 0.6 * math.exp(0.0)
NWB = 5
MIXSTOP = [0]


def build(stages=("ffn1", "mix", "ffn2"), nseq=2, debug=False):
    nc = bass.Bass("TRN2", target_bir_lowering=False)

    def din(name, shape, dt=F32):
        return nc.dram_tensor(name, list(shape), dt, kind="ExternalInput").ap()

    xT = din("xT", [2, D, T])
    cT = din("cT", [128, 8, 2])
    pos = din("pos", [2, T], I32)
    w_ada = din("w_ada", [D, 9 * D])
    b_adaT = din("b_adaT", [128, 72])
    gains = din("gains", [128, 4, 8])
    wgs = [din("wg1", [D, FF]), din("wg2", [D, FF])]
    wus = [din("wu1", [D, FF]), din("wu2", [D, FF])]
    wds = [din("wd1", [FF, D]), din("wd2", [FF, D])]
    w_in = din("w_in", [D, 2208])
    w_out = din("w_out", [D, D])
    w_uq = din("w_uq", [384, 384])
    w_ukv = din("w_ukv", [256, 768])
    lamv = din("lamv", [4, 64])
    gsmall = din("gsmall", [128, 10])
    cmats = din("cmats", [4, 128, 128])
    invf = din("invf", [128, 2])
    outT = nc.dram_tensor("outT", [2, D, T], F32, kind="ExternalOutput").ap()
    dbg = nc.dram_tensor("dbg", [128, 144], F32, kind="ExternalOutput").ap() if debug else None

    with contextlib.ExitStack() as es:
        S = Sched(nc)

        def sb(name, shape, dt):
            return es.enter_context(nc.sbuf_tensor(name, list(shape), dt))

        def tile(name, shape, dt):
            return sb(name, shape, dt), Res(name)

        hT = sb("hT", [128, NK, T], F32)
        hR = [[Res(f"h{k}_{t}") for t in range(NTT)] for k in range(NK)]
        uT = sb("uT", [128, NK, T], BF16)
        uR = [[Res(f"u{k}_{t}") for t in range(NTT)] for k in range(NK)]
        BIG = [sb("bigA", [128, 8256], BF16), sb("bigB", [128, 8256], BF16)]
        bigR = [[Res("A")], [Res("Blo"), Res("Bhi")]]
        WB = [sb(f"wb{i}", [128, 4096], BF16) for i in range(NWB)]
        WR = [Res(f"wb{i}") for i in range(NWB)]
        PSALL = es.enter_context(nc.psum_tensor("psall", [128, 4096], F32))
        PS = [PSALL[:, i * 512:(i + 1) * 512] for i in range(8)]
        PR = [Res(f"ps{i}", excl=True) for i in range(8)]

        class Rot:
            def __init__(self, items):
                self.items = items
                self.i = 0

            def next(self):
                x = self.items[self.i % len(self.items)]
                self.i += 1
                return x

        ftmp = Rot([tile(f"ft{i}", [128, 512], F32) for i in range(4)])
        rstmp = Rot([tile(f"rs{i}", [128, 512], F32) for i in range(2)])
        btmp = Rot([tile(f"bt{i}", [128, 512], BF16) for i in range(4)])
        ptbuf = sb("ptbuf", [128, 2048], BF16)
        ptR4 = [Res(f"pt{i}") for i in range(4)]

        class PtRot:
            def __init__(self):
                self.i = 0
                self.wide = False

            def next(self):
                k = self.i
                self.i += 1
                if self.wide:
                    k %= 2
                    return ptbuf[:, k * 1024:(k + 1) * 1024], ptR4[k]
                k %= 4
                return ptbuf[:, k * 512:(k + 1) * 512], ptR4[k]
        ptp = PtRot()
        itmp, itmpR = tile("itmp", [128, 512], I32)
        posi, posiR = tile("posi", [128, 512], I32)
        accsb, accsbR = tile("accsb", [128, 4, 258], F32)
        osb = Rot([tile(f"osb{i}", [128, 128], F32) for i in range(4)])
        obf = Rot([tile(f"obf{i}", [128, 128], BF16) for i in range(4)])
        junk, junkR = tile("junk", [128, 128], F32)
        small, smallR = tile("small", [128, 32], F32)

        cm_sb, cmR = tile("cm_sb", [128, 4, 128], BF16)
        ones_sb, onesR = tile("ones_sb", [128, 128], BF16)
        invf_sb, invfR = tile("invf_sb", [128, 2], F32)
        gains_sb, gainsR = tile("gains_sb", [128, 4, 8], F32)
        gsm_sb, gsmR = tile("gsm_sb", [128, 10], F32)
        bada_sb, badaR = tile("bada_sb", [128, 72], F32)
        cT_sb, cTR = tile("cT_sb", [128, 8, 2], F32)
        cab, cabR = tile("cab", [128, 8, 2], BF16)
        modT, modR = tile("modT", [128, 72, 2], F32)
        SC, scR = tile("SC", [128, 3, 3, 2, 8], F32)
        lam_sb, lamR = tile("lam_sb", [128, 4, 64], F32)
        lamc, lamcR = tile("lamc", [128, 8], F32)

        ident = cm_sb[:, 0, :]
        rdiff = cm_sb[:, 1, :]
        rmla = cm_sb[:, 2, :]
        maskT = cm_sb[:, 3, :]

        psall = Rot(list(range(8)))

        I = S.op
        AX = mybir.AxisListType.X
        S.dma("pool", w=[cmR], stream="c0", out=cm_sb[:], in_=cmats.rearrange("c p n -> p c n"))
        S.dma("sp", w=[invfR], stream="c1", out=invf_sb[:], in_=invf)
        S.dma("sp", w=[gainsR], stream="c2", out=gains_sb[:], in_=gains)
        S.dma("sp", w=[gsmR], stream="c3", out=gsm_sb[:], in_=gsmall)
        S.dma("sp", w=[badaR], stream="c4", out=bada_sb[:], in_=b_adaT)
        S.dma("sp", w=[cTR], stream="c5", out=cT_sb[:], in_=cT)
        for i in range(4):
            S.dma("sp", w=[lamR], stream="c6", out=lam_sb[:, i, :], in_=lamv[i:i + 1, :].broadcast_to([128, 64]))
        I("dve", "memset", w=[onesR], ap=ones_sb[:], constant=1.0)
        I("dve", "tensor_tensor", r=[lamR], w=[lamR], out=lam_sb[:, 0, :], in0=lam_sb[:, 0, :], in1=lam_sb[:, 1, :], op=ALU.mult)
        I("dve", "tensor_tensor", r=[lamR], w=[lamR], out=lam_sb[:, 2, :], in0=lam_sb[:, 2, :], in1=lam_sb[:, 3, :], op=ALU.mult)
        I("dve", "reduce_sum", r=[lamR], w=[lamcR], out=lamc[:, 1:2], in_=lam_sb[:, 0, :], axis=AX)
        I("dve", "reduce_sum", r=[lamR], w=[lamcR], out=lamc[:, 2:3], in_=lam_sb[:, 2, :], axis=AX)
        I("act", "activation", r=[lamcR], w=[lamcR], out=lamc[:, 3:5], in_=lamc[:, 1:3], func=AF.Exp)
        I("dve", "scalar_tensor_tensor", r=[lamcR], w=[lamcR], out=lamc[:, 0:1], in0=lamc[:, 4:5], scalar=-LAM_INIT, in1=lamc[:, 3:4],
          op0=ALU.add, op1=ALU.subtract)

        class WStream:
            def __init__(self):
                self.free = list(range(NWB))
                self.pending = []
                self.loaded = {}

            def plan(self, key, loadfn):
                self.pending.append((key, loadfn))

            def pump(self):
                while self.pending and self.free:
                    key, fn = self.pending.pop(0)
                    b = self.free.pop(0)
                    self.loaded[key] = b
                    if fn is not None:
                        fn(b)

            def get(self, key):
                self.pump()
                assert key in self.loaded, ("weight tile not loaded", key, list(self.loaded), self.pending[:3])
                return self.loaded[key]

            def release(self, key):
                self.free.append(self.loaded.pop(key))
                self.pump()

        ws = WStream()

        def wload(key, parts):
            def fn(b):
                for dst_fn, src in parts:
                    S.dma("pool", w=[WR[b]], stream=f"w{b}", out=dst_fn(b), in_=src)
            ws.plan(key, fn)

        def wview(b, n, k):
            return WB[b][:, 0:n * k].rearrange("p (k n) -> p k n", k=k)

        I("act", "activation", r=[cTR], w=[cTR], out=cT_sb[:], in_=cT_sb[:], func=AF.Silu)
        I("dve", "tensor_copy", r=[cTR], w=[cabR], out=cab[:], in_=cT_sb[:])

        def plan_ada(j):
            wload(("ada", j), [(lambda b: wview(b, 512, 8), w_ada[:, j * 512:(j + 1) * 512].rearrange("(k p) n -> p k n", p=128))])

        def ada_block(j):
            key = ("ada", j)
            b = ws.get(key)
            wv = wview(b, 512, 8)
            pb = psall.next()
            for cc in range(4):
                for k in range(NK):
                    I("pe", "matmul", r=[WR[b], cabR], w=[PR[pb]], out=PS[pb][:, cc * 2:cc * 2 + 2], lhsT=wv[:, k, cc * 128:(cc + 1) * 128],
                      rhs=cab[:, k, :], start=(k == 0), stop=(k == NK - 1))
            for cc in range(4):
                n = 4 * j + cc
                I("dve", "tensor_scalar", r=[PR[pb], badaR], w=[modR], out=modT[:, n, :], in0=PS[pb][:, cc * 2:cc * 2 + 2],
                  scalar1=bada_sb[:, n:n + 1], scalar2=None, op0=ALU.add)
            ws.release(key)

        def ada_derive(subs, parts=("sb", "g")):
            coef = [0.5, 1.0, 0.5]
            for i in subs:
                for b in range(2):
                    if "sb" in parts:
                        I("dve", "scalar_tensor_tensor", r=[modR, gainsR], w=[scR], out=SC[:, i, 0, b, :],
                          in0=modT[:, (3 * i + 1) * 8:(3 * i + 2) * 8, b], scalar=1.0, in1=gains_sb[:, i, :], op0=ALU.add, op1=ALU.mult)
                        I("dve", "tensor_copy", r=[modR], w=[scR], out=SC[:, i, 1, b, :], in_=modT[:, (3 * i) * 8:(3 * i + 1) * 8, b])
                    if "g" in parts:
                        I("dve", "tensor_scalar", r=[modR], w=[scR], out=SC[:, i, 2, b, :], in0=modT[:, (3 * i + 2) * 8:(3 * i + 3) * 8, b],
                          scalar1=coef[i], scalar2=None, op0=ALU.mult)

        def tsl(tt):
            return slice(tt * 512, (tt + 1) * 512)

        def stats(srcs, split=True):
            pb = psall.next()
            n = len(srcs)
            for k, (ap, res) in enumerate(srcs):
                sq, sqR = btmp.next()
                if split and k % 2 == 1:
                    I("dve", "tensor_tensor", r=[res], w=[sqR], out=sq[:], in0=ap, in1=ap, op=ALU.mult)
                else:
                    I("act", "activation", r=[res], w=[sqR], out=sq[:], in_=ap, func=AF.Square)
                I("pe", "matmul", r=[sqR, onesR], w=[PR[pb]], out=PS[pb][:], lhsT=ones_sb[:], rhs=sq[:], start=(k == 0), stop=(k == n - 1))
            return pb

        def finish(pb, dim):
            rs, rsR = rstmp.next()
            I("dve", "tensor_scalar", r=[PR[pb]], w=[rsR], out=rs[:], in0=PS[pb][:], scalar1=1.0 / dim, scalar2=EPS, op0=ALU.mult, op1=ALU.add)
            I("act", "activation", r=[rsR], w=[rsR], out=rs[:], in_=rs[:], func=AF.Ln)
            I("act", "activation", r=[rsR], w=[rsR], out=rs[:], in_=rs[:], func=AF.Exp, scale=-0.5)
            return rs, rsR

        def rstd_bc(srcs, dim, split=True):
            return finish(stats(srcs, split), dim)

        def pipelined_norm(apply):
            hs = lambda tt: [(hT[:, k, tsl(tt)], hR[k][tt]) for k in range(NK)]
            p0 = stats(hs(0), False)
            p1 = stats(hs(1), False)
            r0 = finish(p0, D)
            apply(0, *r0)
            p2 = stats(hs(2), False)
            r1 = finish(p1, D)
            apply(1, *r1)
            p3 = stats(hs(3), False)
            r2 = finish(p2, D)
            apply(2, *r2)
            r3 = finish(p3, D)
            apply(3, *r3)

        def norm_mod(sub, b):
            def apply(tt, rs, rsR):
                for k in range(NK):
                    t1, t1R = ftmp.next()
                    I("dve", "tensor_tensor", r=[hR[k][tt], rsR], w=[t1R], out=t1[:], in0=hT[:, k, tsl(tt)], in1=rs[:], op=ALU.mult)
                    I("act", "activation", r=[t1R, scR], w=[uR[k][tt]], out=uT[:, k, tsl(tt)], in_=t1[:], func=AF.Identity,
                      bias=SC[:, sub, 1, b, k:k + 1], scale=SC[:, sub, 0, b, k:k + 1])
            pipelined_norm(apply)

        uid = [0]

        def plan_ffn(fi, plan_hook=None):
            wg, wu, wd = wgs[fi], wus[fi], wds[fi]
            uid[0] += 1
            oid = uid[0]
            order = []
            for g in range(len(FFN_GROUPS)):
                order.append(("A", g))
                if g >= 1:
                    order.append(("B", g - 1))
            order.append(("B", len(FFN_GROUPS) - 1))
            for si, (kind, g) in enumerate(order):
                f0, f1 = FFN_GROUPS[g]
                nf = f1 - f0
                if kind == "A":
                    for nm, wsrc in (("g", wg), ("u", wu)):
                        wload(("ffn", oid, nm, g), [(lambda b, nf=nf: wview(b, nf * 128, 8),
                                                     wsrc[:, f0 * 128:f1 * 128].rearrange("(k p) n -> p k n", p=128))])
                else:
                    wload(("ffn", oid, "d", g), [(lambda b, nf=nf: wview(b, 1024, nf),
                                                  wd[f0 * 128:f1 * 128, :].rearrange("(f p) n -> p f n", p=128))])
                if plan_hook is not None:
                    plan_hook(si)
            return oid, order

        def ffn(sub, b, plan, hook=None):
            oid, order = plan
            last_g = len(FFN_GROUPS) - 1
            for si, (kind, g) in enumerate(order):
                f0, f1 = FFN_GROUPS[g]
                nf = f1 - f0
                big = g % 2
                actv = BIG[big][:, 0:8192].rearrange("p (f t) -> p f t", f=4)
                if kind == "A":
                    kg, ku = ("ffn", oid, "g", g), ("ffn", oid, "u", g)
                    bg, bu = ws.get(kg), ws.get(ku)
                    wgv = wview(bg, nf * 128, 8)
                    wuv = wview(bu, nf * 128, 8)

                    def gate_up(f, tts):
                        pg = {tt: psall.next() for tt in tts}
                        for k in range(NK):
                            for tt in tts:
                                I("pe", "matmul", r=[WR[bg], uR[k][tt]], w=[PR[pg[tt]]], out=PS[pg[tt]][:], lhsT=wgv[:, k, f * 128:(f + 1) * 128],
                                  rhs=uT[:, k, tsl(tt)], start=(k == 0), stop=(k == NK - 1))
                        pu = {tt: psall.next() for tt in tts}
                        for k in range(NK):
                            for tt in tts:
                                I("pe", "matmul", r=[WR[bu], uR[k][tt]], w=[PR[pu[tt]]], out=PS[pu[tt]][:], lhsT=wuv[:, k, f * 128:(f + 1) * 128],
                                  rhs=uT[:, k, tsl(tt)], start=(k == 0), stop=(k == NK - 1))
                        sgs = {}
                        for tt in tts:
                            sg, sgR = ftmp.next()
                            sgs[tt] = (sg, sgR)
                            I("act", "activation", r=[PR[pg[tt]]], w=[sgR], out=sg[:], in_=PS[pg[tt]][:], func=AF.Silu)
                        for tt in tts:
                            sg, sgR = sgs[tt]
                            I("dve", "tensor_tensor", r=[sgR, PR[pu[tt]]], w=bigR[big], out=actv[:, f, tsl(tt)], in0=sg[:], in1=PS[pu[tt]][:], op=ALU.mult)

                    if g == 0:
                        for tt in range(NTT):
                            for f in range(nf):
                                gate_up(f, [tt])
                    else:
                        for f in range(nf):
                            gate_up(f, list(range(NTT)))
                    ws.release(kg)
                    ws.release(ku)
                else:
                    kd = ("ffn", oid, "d", g)
                    bd = ws.get(kd)
                    wdv = wview(bd, 1024, nf)

                    def down(m, tts):
                        po = {tt: psall.next() for tt in tts}
                        for f in range(nf):
                            for tt in tts:
                                I("pe", "matmul", r=[WR[bd]] + bigR[big], w=[PR[po[tt]]], out=PS[po[tt]][:], lhsT=wdv[:, f, m * 128:(m + 1) * 128],
                                  rhs=actv[:, f, tsl(tt)], start=(f == 0), stop=(f == nf - 1))
                        for tt in tts:
                            I("dve", "scalar_tensor_tensor", r=[PR[po[tt]], scR, hR[m][tt]], w=[hR[m][tt]], out=hT[:, m, tsl(tt)], in0=PS[po[tt]][:],
                              scalar=SC[:, sub, 2, b, m:m + 1], in1=hT[:, m, tsl(tt)], op0=ALU.mult, op1=ALU.add)

                    if g == last_g:
                        for tt in range(NTT):
                            for m in range(NK):
                                down(m, [tt])
                    else:
                        for m in range(NK):
                            down(m, list(range(NTT)))
                    ws.release(kd)
                if hook is not None:
                    hook(si)

        def load_x(b):
            for tt in range(NTT):
                grp = [S.dma("sp", w=[hR[k][tt]], stream=f"x{tt}", out=hT[:, k, tsl(tt)], in_=xT[b, k * 128:(k + 1) * 128, tsl(tt)]) for k in range(NK)]
                for o in grp:
                    o.dval = grp[-1].dval

        outs = []

        def final_store(b):
            def apply(tt, rs, rsR):
                for k in range(NK):
                    I("dve", "scalar_tensor_tensor", r=[hR[k][tt], gainsR, rsR], w=[hR[k][tt]], out=hT[:, k, tsl(tt)], in0=hT[:, k, tsl(tt)],
                      scalar=gains_sb[:, 3, k:k + 1], in1=rs[:], op0=ALU.mult, op1=ALU.mult)
                grp = [S.dma("sp", r=[hR[k][tt]], stream=f"o{tt}", out=outT[b, k * 128:(k + 1) * 128, tsl(tt)], in_=hT[:, k, tsl(tt)]) for k in range(NK)]
                for o in grp:
                    o.dval = grp[-1].dval
                outs.extend(grp)
            pipelined_norm(apply)

        Ares, Blo, Bhi = bigR[0][0], bigR[1][0], bigR[1][1]
        qv = BIG[0][:, 0:4096].rearrange("p (c t) -> p c t", c=2)
        kv = BIG[0][:, 4096:8192].rearrange("p (c t) -> p c t", c=2)
        vaug = BIG[1][:, 0:4128].rearrange("p (j c d) -> p j c d", j=16, c=2)
        tab = BIG[1][:, 4160:8256].rearrange("p (s t) -> p s t", s=2)
        ps7b = PS[7][:].bitcast(BF16)
        PI_C = 3.1415925

        def plan_mixer():
            uid[0] += 1
            oid = uid[0]
            for a in range(2):
                wload(("qk", oid, a), [
                    (lambda b: wview(b, 512, 8)[:, :, 0:256], w_in[:, a * 256:(a + 1) * 256].rearrange("(k p) n -> p k n", p=128)),
                    (lambda b: wview(b, 512, 8)[:, :, 256:512], w_in[:, 512 + a * 256:512 + (a + 1) * 256].rearrange("(k p) n -> p k n", p=128))])
                wload(("v", oid, a), [(lambda b: wview(b, 256, 8), w_in[:, 1024 + a * 256:1024 + (a + 1) * 256].rearrange("(k p) n -> p k n", p=128))])
                ws.plan(("mixT", oid, a), None)
                wload(("wo", oid, a), [(lambda b: wview(b, 1024, 2), w_out[a * 256:(a + 1) * 256, :].rearrange("(f p) n -> p f n", p=128))])
            wload(("lat1", oid), [(lambda b: wview(b, 384, 8), w_in[:, 1536:1920].rearrange("(k p) n -> p k n", p=128))])
            wload(("lat2", oid), [(lambda b: wview(b, 288, 8), w_in[:, 1920:2208].rearrange("(k p) n -> p k n", p=128))])
            wload(("wmla", oid), [
                (lambda b: WB[b][:, 0:1152].rearrange("p (k n) -> p k n", k=3), w_uq.rearrange("(k p) n -> p k n", p=128)),
                (lambda b: WB[b][:, 1152:2688].rearrange("p (k n) -> p k n", k=2), w_ukv.rearrange("(k p) n -> p k n", p=128))])
            ws.plan(("omT", oid, 0), None)
            ws.plan(("omT", oid, 1), None)
            wload(("wo", oid, "mla"), [(lambda b: wview(b, 1024, 4), w_out[512:1024, :].rearrange("(f p) n -> p f n", p=128))])
            return oid

        def tables(b, col, rows, tts=range(NTT)):
            for tt in tts:
                S.dma("sp", w=[posiR], stream="pos", out=posi[:], in_=pos[b:b + 1, tsl(tt)].broadcast_to([128, 512]))
                posf, posfR = ftmp.next()
                I("dve", "tensor_copy", r=[posiR], w=[posfR], out=posf[:], in_=posi[:])
                ang, angR = ftmp.next()
                I("dve", "tensor_scalar", r=[posfR, invfR], w=[angR], out=ang[:], in0=posf[:], scalar1=invf_sb[:, col:col + 1], scalar2=None, op0=ALU.mult)
                for si, shift in ((1, 0.0), (0, 0.5 * math.pi)):
                    if shift != 0.0:
                        a2, a2R = ftmp.next()
                        I("dve", "tensor_scalar", r=[angR], w=[a2R], out=a2[:], in0=ang[:], scalar1=shift, scalar2=None, op0=ALU.add)
                    else:
                        a2, a2R = ang, angR
                    I("dve", "tensor_scalar", r=[a2R], w=[itmpR], out=itmp[:], in0=a2[:], scalar1=1.0 / TWO_PI, scalar2=None, op0=ALU.mult)
                    kf, kfR = ftmp.next()
                    I("dve", "tensor_copy", r=[itmpR], w=[kfR], out=kf[:], in_=itmp[:])
                    y, yR = ftmp.next()
                    I("dve", "scalar_tensor_tensor", r=[kfR, a2R], w=[yR], out=y[:], in0=kf[:], scalar=-C1, in1=a2[:], op0=ALU.mult, op1=ALU.add)
                    I("dve", "scalar_tensor_tensor", r=[kfR, yR], w=[yR], out=y[:], in0=kf[:], scalar=-C2, in1=y[:], op0=ALU.mult, op1=ALU.add)
                    I("dve", "tensor_scalar", r=[yR], w=[yR], out=y[:], in0=y[:], scalar1=-PI_C, scalar2=PI_C, op0=ALU.max, op1=ALU.min)
                    I("act", "activation", r=[yR], w=[Bhi], out=tab[rows, si, tsl(tt)], in_=y[rows, :], func=AF.Sin)

        pend = []

        def defer(fn):
            if pend:
                pend.pop(0)()
            pend.append(fn)

        def flush():
            while pend:
                pend.pop(0)()

        def rope(psrc, srcR, rows, rmat, dst, dstR, tt, deferred=True):
            qb, qbR = btmp.next()
            I("act", "activation", r=[srcR], w=[qbR], out=qb[rows, :], in_=PS[psrc][rows, :], func=AF.Identity)

            def partB():
                pr = psall.next()
                M = rows.stop
                I("pe", "matmul", r=[qbR, cmR], w=[PR[pr]], out=PS[pr][0:M, :], lhsT=rmat[rows, 0:M], rhs=qb[rows, :], start=True, stop=True)
                t1, t1R = ftmp.next()
                I("dve", "tensor_tensor", r=[srcR, Bhi], w=[t1R], out=t1[rows, :], in0=tab[rows, 0, tsl(tt)], in1=PS[psrc][rows, :], op=ALU.mult)
                t2, t2R = ftmp.next()
                I("dve", "tensor_tensor", r=[PR[pr], Bhi], w=[t2R], out=t2[rows, :], in0=tab[rows, 1, tsl(tt)], in1=PS[pr][rows, :], op=ALU.mult)
                I("dve", "tensor_tensor", r=[t1R, t2R], w=[dstR], out=dst, in0=t1[rows, :], in1=t2[rows, :], op=ALU.add)
            if deferred:
                defer(partB)
            else:
                partB()

        evq = []

        def tick(n):
            for _ in range(min(n, len(evq))):
                evq.pop(0)()

        stb = Rot([4, 5, 6, 7])
        psb = {i: PS[i][:].bitcast(BF16) for i in (4, 5, 6, 7)}

        stpair = Rot([4, 6])

        def attention_pass(groups, scale, evac, share):
            ptp.wide = not share
            nh = 2 if share else 1
            bph = 4 // nh
            W = bph * 128
            L = 3 if share else 1
            steps = []
            for g in range(len(groups)):
                for qc in range(4):
                    for j in range(4 * qc + 4):
                        r = max(0, j - 4 * qc)
                        for h in range(nh):
                            blk0 = max(bph * h, r)
                            nb = bph * (h + 1) - blk0
                            if nb > 0:
                                steps.append((g, qc, j, h, blk0, nb))
            lasts = {}
            for idx, (g, qc, j, h, blk0, nb) in enumerate(steps):
                lasts[(g, qc)] = idx
            state = {}

            def qk(idx):
                g, qc, j, h, blk0, nb = steps[idx]
                r = j - 4 * qc
                q0 = qc * 512 + blk0 * 128
                nq = nb * 128
                if share:
                    st = stb.next()
                    res = [PR[st]]
                    base = st * 512
                else:
                    st = stpair.next()
                    res = [PR[st], PR[st + 1]]
                    base = st * 512
                diag = (r >= 0 and blk0 == r)
                for m, (c, rs_) in enumerate(groups[g]):
                    I("pe", "matmul", r=[Ares], w=res, out=PSALL[:, base + m * W:base + m * W + nq], lhsT=kv[rs_, c, j * 128:(j + 1) * 128],
                      rhs=qv[rs_, c, q0:q0 + nq], start=(m == 0 or not share), stop=(not diag), skip_group_check=True)
                if diag:
                    for m in range(2):
                        I("pe", "matmul", r=[cmR], w=res, out=PSALL[:, base + m * W:base + m * W + 128], lhsT=ident, rhs=maskT,
                          start=False, stop=True, skip_group_check=True)
                pt, ptR = ptp.next()
                I("act", "activation", r=res, w=[ptR], out=pt[:, 0:2 * W].rearrange("p (m n) -> p m n", m=2)[:, :, 0:nq],
                  in_=PSALL[:, base:base + 2 * W].rearrange("p (m n) -> p m n", m=2)[:, :, 0:nq], func=AF.Exp, scale=scale)
                state[idx] = (pt, ptR)

            def pv(idx):
                g, qc, j, h, blk0, nb = steps[idx]
                pt, ptR = state.pop(idx)
                for m, (c, rs_) in enumerate(groups[g]):
                    for bi in range(nb):
                        i = blk0 + bi
                        bank, off = m * 2 + i // 2, (i % 2) * 129
                        I("pe", "matmul", r=[ptR, Blo], w=[PR[bank]], out=PS[bank][:, off:off + 129], lhsT=pt[:, m * W + bi * 128:m * W + (bi + 1) * 128],
                          rhs=vaug[:, j, c, :], start=(j == 0 and i % 2 == 0), stop=(j == 4 * qc + i), skip_group_check=True)
                if lasts[(g, qc)] == idx:
                    evac(g, qc)

            n = len(steps)
            for i in range(min(L, n)):
                qk(i)
            for i in range(n):
                if i + L < n:
                    qk(i + L)
                pv(i)
                tick(1)
            tick(len(evq))

        accv = accsb[:].rearrange("p j (i d) -> p (j i) d", i=2)

        def transpose_out(srcs, dstap, dstR):
            tb_ = stb.next()
            for i, (ap, res) in enumerate(srcs):
                I("pe", "transpose", r=[res, cmR], w=[PR[tb_]], out=psb[tb_][:, i * 128:(i + 1) * 128], in_=ap, identity=ident)
            I("dve", "tensor_copy", r=[PR[tb_]], w=[dstR], out=dstap, in_=psb[tb_][:, 0:512])

        def out_proj(b, bo, nch, srcv, srcR):
            wov = wview(bo, 1024, nch)
            for m in range(NK):
                po = [psall.next() for _ in range(NTT)]
                for j in range(nch):
                    for tt in range(NTT):
                        I("pe", "matmul", r=[WR[bo], srcR], w=[PR[po[tt]]], out=PS[po[tt]][:], lhsT=wov[:, j, m * 128:(m + 1) * 128],
                          rhs=srcv[:, j, tsl(tt)], start=(j == 0), stop=(j == nch - 1))
                for tt in range(NTT):
                    I("dve", "scalar_tensor_tensor", r=[PR[po[tt]], scR, hR[m][tt]], w=[hR[m][tt]], out=hT[:, m, tsl(tt)], in0=PS[po[tt]][:],
                      scalar=SC[:, 1, 2, b, m:m + 1], in1=hT[:, m, tsl(tt)], op0=ALU.mult, op1=ALU.add)

        def mixer(b, oid):
            full = slice(0, 128)
            for a in range(2):
                bqk, bv = ws.get(("qk", oid, a)), ws.get(("v", oid, a))
                wqk = wview(bqk, 512, 8)
                wv = wview(bv, 256, 8)
                I("dve", "memset", w=[Blo], ap=vaug[:, :, :, 128:129], constant=1.0)
                if MIXSTOP[0] == 21:
                    return
                for tt in range(NTT):
                    if a == 0:
                        tables(b, 0, full, [tt])
                    for c in range(2):
                        for which, dstv in ((0, qv), (1, kv)):
                            pq = psall.next()
                            for k in range(NK):
                                I("pe", "matmul", r=[WR[bqk], uR[k][tt]], w=[PR[pq]], out=PS[pq][:], lhsT=wqk[:, k, which * 256 + c * 128:which * 256 + (c + 1) * 128],
                                  rhs=uT[:, k, tsl(tt)], start=(k == 0), stop=(k == NK - 1))
                            rope(pq, PR[pq], full, rdiff, dstv[:, c, tsl(tt)], Ares, tt)
                    for tb in range(4):
                        blk = tt * 4 + tb
                        pv = psall.next()
                        for k in range(NK):
                            I("pe", "matmul", r=[WR[bv], uR[k][tt]], w=[PR[pv]], out=PS[pv][:, 0:256], lhsT=uT[:, k, blk * 128:(blk + 1) * 128],
                              rhs=wv[:, k, :], start=(k == 0), stop=(k == NK - 1))
                        I("act", "activation", r=[PR[pv]], w=[Blo], out=vaug[:, blk, :, 0:128], in_=PS[pv][:, 0:256].rearrange("p (c d) -> p c d", c=2), func=AF.Identity)
                flush()
                ws.release(("qk", oid, a))
                ws.release(("v", oid, a))
                bm = ws.get(("mixT", oid, a))
                mixv = wview(bm, 2048, 2)

                def evac_diff(c, qc, mixv=mixv, bm=bm):
                    tick(len(evq))
                    for j in range(4):
                        I("dve", "tensor_copy", r=[PR[j]], w=[accsbR], out=accsb[:, j, :], in_=PS[j][:, 0:258])
                    I("dve", "reciprocal", r=[accsbR], w=[smallR], out=small[:, 0:8], in_=accv[:, :, 128])
                    I("dve", "tensor_scalar", r=[smallR, lamcR], w=[smallR], out=small[:, 4:8], in0=small[:, 4:8], scalar1=lamc[:, 0:1], scalar2=None, op0=ALU.mult)
                    I("dve", "memset", w=[smallR], ap=small[:, 8:12], constant=0.0)

                    def blk(i):
                        t, tR = osb.next()
                        I("dve", "tensor_scalar", r=[accsbR, smallR], w=[tR], out=t[:], in0=accv[:, 4 + i, 0:128], scalar1=small[:, 4 + i:5 + i], scalar2=None, op0=ALU.mult)
                        I("dve", "scalar_tensor_tensor", r=[accsbR, smallR, tR], w=[accsbR], out=accv[:, i, 0:128], in0=accv[:, i, 0:128],
                          scalar=small[:, i:i + 1], in1=t[:], op0=ALU.mult, op1=ALU.add)
                        I("dve", "tensor_tensor", r=[accsbR], w=[junkR], out=junk[:], in0=accv[:, i, 0:128], in1=accv[:, i, 0:128], op=ALU.mult)
                        I("dve", "reduce_sum", r=[junkR], w=[smallR], out=small[:, 8 + i:9 + i], in_=junk[:], axis=AX)
                    for i in range(4):
                        evq.append(lambda i=i: blk(i))

                    def rstd_a():
                        I("dve", "tensor_scalar", r=[smallR], w=[smallR], out=small[:, 8:12], in0=small[:, 8:12], scalar1=1.0 / 128, scalar2=EPS, op0=ALU.mult, op1=ALU.add)

                    def rstd_b():
                        I("act", "activation", r=[smallR], w=[smallR], out=small[:, 8:12], in_=small[:, 8:12], func=AF.Ln)
                        I("act", "activation", r=[smallR], w=[smallR], out=small[:, 8:12], in_=small[:, 8:12], func=AF.Exp, scale=-0.5)
                    evq.append(rstd_a)
                    evq.append(lambda: None)
                    evq.append(rstd_b)
                    srcs = []

                    def scale_out():
                        for i in range(4):
                            ob, obR = obf.next()
                            I("dve", "tensor_scalar", r=[accsbR, smallR], w=[obR], out=ob[:], in0=accv[:, i, 0:128], scalar1=small[:, 8 + i:9 + i], scalar2=None, op0=ALU.mult)
                            srcs.append((ob[:], obR))
                    evq.append(scale_out)
                    evq.append(lambda: transpose_out(srcs, mixv[:, c, tsl(qc)], WR[bm]))

                attention_pass([[(c, slice(0, 64)), (c, slice(64, 128))] for c in range(2)], 0.125, evac_diff, False)
                if MIXSTOP[0] == 3:
                    return
                bo = ws.get(("wo", oid, a))
                wov = wview(bo, 1024, 2)
                I("dve", "tensor_scalar", r=[WR[bo], gsmR], w=[WR[bo]], out=wov[:], in0=wov[:], scalar1=gsm_sb[:, 0:1], scalar2=1.0 - LAM_INIT, op0=ALU.mult, op1=ALU.mult)
                out_proj(b, bo, 2, mixv, WR[bm])
                ws.release(("mixT", oid, a))
                ws.release(("wo", oid, a))

            r96 = slice(64, 96)
            b1, b2, bw = ws.get(("lat1", oid)), ws.get(("lat2", oid)), ws.get(("wmla", oid))
            l1 = wview(b1, 384, 8)
            l2 = wview(b2, 288, 8)
            wuq = WB[bw][:, 0:1152].rearrange("p (k n) -> p k n", k=3)
            wukv = WB[bw][:, 1152:2688].rearrange("p (k n) -> p k n", k=2)
            wukv4 = WB[bw][:, 1152:2688].rearrange("p (k h n) -> p k h n", k=2, h=4)
            for tt in range(NTT):
                tables(b, 1, r96, [tt])
                pc = [psall.next() for _ in range(3)]
                for j in range(3):
                    for k in range(NK):
                        I("pe", "matmul", r=[WR[b1], uR[k][tt]], w=[PR[pc[j]]], out=PS[pc[j]][:], lhsT=l1[:, k, j * 128:(j + 1) * 128], rhs=uT[:, k, tsl(tt)],
                          start=(k == 0), stop=(k == NK - 1))
                pk = [psall.next() for _ in range(2)]
                for j in range(2):
                    for k in range(NK):
                        I("pe", "matmul", r=[WR[b2], uR[k][tt]], w=[PR[pk[j]]], out=PS[pk[j]][:], lhsT=l2[:, k, j * 128:(j + 1) * 128], rhs=uT[:, k, tsl(tt)],
                          start=(k == 0), stop=(k == NK - 1))
                pkr = psall.next()
                for k in range(NK):
                    I("pe", "matmul", r=[WR[b2], uR[k][tt]], w=[PR[pkr]], out=PS[pkr][0:96, :], lhsT=l2[:, k, 192:288], rhs=uT[:, k, tsl(tt)],
                      start=(k == 0), stop=(k == NK - 1))
                rs, rsR = rstd_bc([(PS[pc[j]][:], PR[pc[j]]) for j in range(3)], 384, split=False)
                for j in range(3):
                    I("dve", "scalar_tensor_tensor", r=[PR[pc[j]], gsmR, rsR], w=[uR[j][tt]], out=uT[:, j, tsl(tt)], in0=PS[pc[j]][:], scalar=gsm_sb[:, 1 + j:2 + j],
                      in1=rs[:], op0=ALU.mult, op1=ALU.mult)
                rs, rsR = rstd_bc([(PS[pk[j]][:], PR[pk[j]]) for j in range(2)], 256, split=False)
                for j in range(2):
                    I("dve", "scalar_tensor_tensor", r=[PR[pk[j]], gsmR, rsR], w=[uR[3 + j][tt]], out=uT[:, 3 + j, tsl(tt)], in0=PS[pk[j]][:], scalar=gsm_sb[:, 4 + j:5 + j],
                      in1=rs[:], op0=ALU.mult, op1=ALU.mult)
                rope(pkr, PR[pkr], r96, rmla, uT[r96, 5, tsl(tt)], uR[5][tt], tt, deferred=False)
            ws.release(("lat1", oid))
            ws.release(("lat2", oid))

            oms = []
            for a in range(2):
                I("dve", "memset", w=[Blo], ap=vaug[:, :, :, 128:129], constant=1.0)
                for tt in range(NTT):
                    for c in range(2):
                        hq = 2 * a + c
                        pq = psall.next()
                        for j in range(3):
                            I("pe", "matmul", r=[WR[bw], uR[j][tt]], w=[PR[pq]], out=PS[pq][0:96, :], lhsT=wuq[:, j, hq * 96:(hq + 1) * 96], rhs=uT[:, j, tsl(tt)],
                              start=(j == 0), stop=(j == 2))
                        I("act", "activation", r=[PR[pq]], w=[Ares], out=qv[0:64, c, tsl(tt)], in_=PS[pq][0:64, :], func=AF.Identity)
                        rope(pq, PR[pq], r96, rmla, qv[r96, c, tsl(tt)], Ares, tt)
                        pkk = psall.next()
                        for j in range(2):
                            I("pe", "matmul", r=[WR[bw], uR[3 + j][tt]], w=[PR[pkk]], out=PS[pkk][0:64, :], lhsT=wukv[:, j, hq * 192:hq * 192 + 64], rhs=uT[:, 3 + j, tsl(tt)],
                              start=(j == 0), stop=(j == 1))
                        I("act", "activation", r=[PR[pkk]], w=[Ares], out=kv[0:64, c, tsl(tt)], in_=PS[pkk][0:64, :], func=AF.Identity)
                        I("dve", "tensor_copy", r=[uR[5][tt]], w=[Ares], out=kv[r96, c, tsl(tt)], in_=uT[r96, 5, tsl(tt)])
                    flush()
                    for tb in range(4):
                        blk = tt * 4 + tb
                        pv = psall.next()
                        for j in range(2):
                            I("pe", "matmul", r=[WR[bw], uR[3 + j][tt]], w=[PR[pv]], out=PS[pv][:, 0:256].rearrange("p (c d) -> p c d", c=2),
                              lhsT=uT[:, 3 + j, blk * 128:(blk + 1) * 128], rhs=wukv4[:, j, 2 * a:2 * a + 2, 64:192], start=(j == 0), stop=(j == 1))
                        I("act", "activation", r=[PR[pv]], w=[Blo], out=vaug[:, blk, :, 0:128], in_=PS[pv][:, 0:256].rearrange("p (c d) -> p c d", c=2), func=AF.Identity)
                flush()
                bm = ws.get(("omT", oid, a))
                omv = wview(bm, 2048, 2)
                oms.append((bm, omv))

                def evac_mla(g, qc, bm=bm, omv=omv):
                    tick(len(evq))
                    for j in range(4):
                        I("dve", "tensor_copy", r=[PR[j]], w=[accsbR], out=accsb[:, j, :], in_=PS[j][:, 0:258])
                    I("dve", "reciprocal", r=[accsbR], w=[smallR], out=small[:, 0:8], in_=accv[:, :, 128])
                    for m in range(2):
                        srcs = []

                        def scale_out(m=m, srcs=srcs):
                            for i in range(4):
                                ob, obR = obf.next()
                                I("dve", "tensor_scalar", r=[accsbR, smallR], w=[obR], out=ob[:], in0=accv[:, m * 4 + i, 0:128], scalar1=small[:, m * 4 + i:m * 4 + i + 1],
                                  scalar2=None, op0=ALU.mult)
                                srcs.append((ob[:], obR))
                        evq.append(scale_out)
                        evq.append(lambda m=m, srcs=srcs: transpose_out(srcs, omv[:, m, tsl(qc)], WR[bm]))

                attention_pass([[(0, slice(0, 96)), (1, slice(0, 96))]], 96.0 ** -0.5, evac_mla, MIXSTOP[0] != 7)
                if MIXSTOP[0] in (6, 7):
                    return
            ws.release(("wmla", oid))
            for tt in range(NTT):
                srcs = [(oms[h // 2][1][:, h % 2, tsl(tt)], WR[oms[h // 2][0]]) for h in range(4)]
                rs, rsR = rstd_bc(srcs, 512)
                for h in range(4):
                    bm, omv = oms[h // 2]
                    I("dve", "tensor_tensor", r=[WR[bm], rsR], w=[WR[bm]], out=omv[:, h % 2, tsl(tt)], in0=omv[:, h % 2, tsl(tt)], in1=rs[:], op=ALU.mult)
            bo = ws.get(("wo", oid, "mla"))
            wov = wview(bo, 1024, 4)
            for j in range(4):
                I("dve", "tensor_scalar", r=[WR[bo], gsmR], w=[WR[bo]], out=wov[:, j, :], in0=wov[:, j, :], scalar1=gsm_sb[:, 6 + j:7 + j], scalar2=None, op0=ALU.mult)
            wov4 = wview(bo, 1024, 4)
            for tt in range(NTT):
                for m in range(NK):
                    po = psall.next()
                    for h in range(4):
                        bm, omv = oms[h // 2]
                        I("pe", "matmul", r=[WR[bo], WR[bm]], w=[PR[po]], out=PS[po][:], lhsT=wov4[:, h, m * 128:(m + 1) * 128],
                          rhs=omv[:, h % 2, tsl(tt)], start=(h == 0), stop=(h == 3))
                    I("dve", "scalar_tensor_tensor", r=[PR[po], scR, hR[m][tt]], w=[hR[m][tt]], out=hT[:, m, tsl(tt)], in0=PS[po][:],
                      scalar=SC[:, 1, 2, b, m:m + 1], in1=hT[:, m, tsl(tt)], op0=ALU.mult, op1=ALU.add)
            ws.release(("omT", oid, 0))
            ws.release(("omT", oid, 1))
            ws.release(("wo", oid, "mla"))

        NADA0 = 4 if "ffn1" in stages else 18
        hook_blocks = {0: [4, 5]}
        for si in range(1, 12):
            hook_blocks[si] = [5 + si]
        for j in range(NADA0):
            plan_ada(j)
        plans = {}
        for s in range(nseq):
            if "ffn1" in stages:
                hookp = (lambda si: [plan_ada(j) for j in hook_blocks[si]]) if s == 0 else None
                plans[(s, 0)] = plan_ffn(0, hookp)
                if s == 0:
                    plan_ada(17)
            if "mix" in stages:
                plans[(s, "mix")] = plan_mixer()
            if "ffn2" in stages:
                plans[(s, 1)] = plan_ffn(1)
        load_x(0)
        for j in range(NADA0):
            ada_block(j)
        if NADA0 == 18:
            ada_derive([0, 1, 2])
        else:
            ada_derive([0], ("sb",))
        for s in range(nseq):
            if s > 0:
                load_x(s)
            if "ffn1" in stages:
                norm_mod(0, s)
                if s == 0:
                    def hook(si):
                        for j in hook_blocks[si]:
                            ada_block(j)
                        if si == 0:
                            ada_derive([0], ("g",))
                    ffn(0, s, plans[(s, 0)], hook=hook)
                    ada_block(17)
                    ada_derive([1, 2])
                else:
                    ffn(0, s, plans[(s, 0)])
            if "mix" in stages:
                norm_mod(1, s)
                mixer(s, plans[(s, "mix")])
            if "ffn2" in stages:
                norm_mod(2, s)
                ffn(2, s, plans[(s, 1)])
            final_store(s)
        if debug:
            outs.append(S.dma("sp", r=[scR], stream="dbg", out=dbg, in_=SC[:].rearrange("p a b c d -> p (a b c d)")))
        S.emit(final_waits=outs)
    return nc


def _host_inputs(inputs):
    f = lambda a: np.ascontiguousarray(np.asarray(a, dtype=np.float32))
    x = np.asarray(inputs["x"], dtype=np.float32)
    c = np.asarray(inputs["c"], dtype=np.float32)
    positions = np.asarray(inputs["positions"]).astype(np.int32)
    col = lambda v, n: np.ascontiguousarray(np.asarray(v, np.float32).reshape(n, 128).T)
    gains = np.stack([col(inputs["ffn1_norm"][0], 8), col(inputs["mix_norm"][0], 8), col(inputs["ffn2_norm"][0], 8),
                      col(inputs["final_norm"], 8)], axis=1)
    gsmall = np.concatenate([col(inputs["diff_subln"][0], 1), col(inputs["mla_q_norm"][0], 3), col(inputs["mla_kv_norm"][0], 2),
                             col(inputs["mla_out_norm"][0], 4)], axis=1)
    lamv = np.stack([inputs["diff_lambda_q1"][0], inputs["diff_lambda_k1"][0], inputs["diff_lambda_q2"][0], inputs["diff_lambda_k2"][0]]).astype(np.float32)
    ident = np.eye(128, dtype=np.float32)
    rdiff = np.zeros((128, 128), np.float32)
    for j in range(128):
        if (j % 64) < 32:
            rdiff[j + 32, j] = -1.0
        else:
            rdiff[j - 32, j] = 1.0
    rmla = np.zeros((128, 128), np.float32)
    for j in range(64, 96):
        if (j - 64) < 16:
            rmla[j + 16, j] = -1.0
        else:
            rmla[j - 16, j] = 1.0
    maskT = np.where(np.arange(128)[None, :] >= np.arange(128)[:, None], 0.0, -30000.0).astype(np.float32)
    cmats = np.stack([ident, rdiff, rmla, maskT])
    invf = np.zeros((128, 2), np.float32)
    p = np.arange(128)
    invf[:, 0] = (10000.0 ** (-(2.0 * (p % 32)) / 64.0)).astype(np.float32)
    invf[:, 1] = (10000.0 ** (-(2.0 * (p % 16)) / 32.0)).astype(np.float32)
    shared = dict(
        w_ada=f(inputs["w_ada"][0]), b_adaT=col(inputs["b_ada"][0], 72), gains=np.ascontiguousarray(gains),
        wg1=f(inputs["ffn1_w_gate"][0]), wu1=f(inputs["ffn1_w_up"][0]), wd1=f(inputs["ffn1_w_down"][0]),
        wg2=f(inputs["ffn2_w_gate"][0]), wu2=f(inputs["ffn2_w_up"][0]), wd2=f(inputs["ffn2_w_down"][0]),
        w_in=f(inputs["w_in"][0]), w_out=f(inputs["w_out"][0]), w_uq=f(inputs["mla_w_uq"][0]), w_ukv=f(inputs["mla_w_ukv"][0]),
        lamv=np.ascontiguousarray(lamv), gsmall=np.ascontiguousarray(gsmall), cmats=cmats, invf=invf)
    maps = []
    for i in range(8):
        bs = slice(2 * i, 2 * i + 2)
        m = dict(shared)
        m["xT"] = np.ascontiguousarray(x[bs].transpose(0, 2, 1))
        m["cT"] = np.ascontiguousarray(c[bs].reshape(2, 8, 128).transpose(2, 1, 0))
        m["pos"] = np.ascontiguousarray(positions[bs])
        maps.append(m)
    return maps


def kernel(**inputs):
    maps = _host_inputs(inputs)
    nc = build()
    res = run_bass_kernel_spmd(nc, maps, core_ids=list(range(8)))
    out = np.empty((16, T, D), np.float32)
    for i in range(8):
        out[2 * i:2 * i + 2] = res.results[i]["outT"].transpose(0, 2, 1)
    return out
```
